# Optimizing a Trainium2 kernel written in Bass

```python
import math
import jax, jax.numpy as jnp
from jax import lax
import numpy as np

D_MODEL = 2048
BATCH = 8
SEQ = 2048
DEPTH = 2

EPS = 1e-6
GRID_W = 64
HEAD_DIM = 128
ATTN_HEADS = D_MODEL // 256
KV_HEADS = max(1, ATTN_HEADS // 4)
ATTN_WIDTH = ATTN_HEADS * HEAD_DIM
KV_WIDTH = KV_HEADS * HEAD_DIM
Q_BLOCK = 128
ROPE_THETA = 10000.0
ROPE_HALF = HEAD_DIM // 2
LRU_WIDTH = D_MODEL // 2
LRU_BLOCKS = 8
LRU_BLOCK = LRU_WIDTH // LRU_BLOCKS
LRU_C = 8.0
CONV_W = 4
CONV_PAD = (2, 1)
EVEN_IN = ATTN_WIDTH + 2 * KV_WIDTH + ATTN_WIDTH + 2 * LRU_WIDTH
EVEN_MIX = ATTN_WIDTH + LRU_WIDTH
MLSTM_HEADS = 8
MLSTM_V_DIM = D_MODEL // MLSTM_HEADS
MLSTM_QK_DIM = MLSTM_V_DIM // 2
MLSTM_WIDTH = MLSTM_HEADS * MLSTM_V_DIM
MLSTM_QK_WIDTH = MLSTM_HEADS * MLSTM_QK_DIM
MLSTM_CHUNK = 128
N_GATE_SETS = 4
ODD_IN = 2 * MLSTM_QK_WIDTH + 3 * MLSTM_WIDTH + N_GATE_SETS * MLSTM_HEADS
N_EVEN = (DEPTH + 1) // 2
N_ODD = DEPTH // 2

kernel_name = 'hybrid_gqa_rglru_mlstm_encoder'


def rmsnorm(x, g):
    xf = x.astype(jnp.float32)
    y = xf * lax.rsqrt(jnp.mean(xf * xf, axis=-1, keepdims=True) + EPS) * g.astype(jnp.float32)
    return y.astype(x.dtype)


def split_at(t, sizes):
    idx = np.cumsum(sizes)[:-1].tolist()
    return jnp.split(t, idx, axis=-1)


def axial_angles(seq_len):
    rows = seq_len // GRID_W
    row = jnp.repeat(jnp.arange(rows), GRID_W).astype(jnp.float32)
    col = jnp.tile(jnp.arange(GRID_W), rows).astype(jnp.float32)
    inv = ROPE_THETA ** (-jnp.arange(0, ROPE_HALF, 2, dtype=jnp.float32) / ROPE_HALF)
    return row[:, None] * inv, col[:, None] * inv


def rotate_half_pairs(x, ang):
    x1, x2 = jnp.split(x, 2, axis=-1)
    c = jnp.cos(ang)[None, :, None, :]
    s = jnp.sin(ang)[None, :, None, :]
    return jnp.concatenate([x1 * c - x2 * s, x1 * s + x2 * c], axis=-1)


def axial_rope(x, ang_row, ang_col):
    xr, xc = jnp.split(x, 2, axis=-1)
    return jnp.concatenate([rotate_half_pairs(xr, ang_row), rotate_half_pairs(xc, ang_col)], axis=-1)


def block_attention(q, k, v):
    B, S, H, D = q.shape
    G = H // KV_HEADS
    nb = S // Q_BLOCK
    qb = q.reshape(B, nb, Q_BLOCK, KV_HEADS, G, D).transpose(1, 0, 3, 4, 2, 5)
    scale = HEAD_DIM ** -0.5

    def one_block(qi):
        s = jnp.einsum('bkgqd,bskd->bkgqs', qi, k) * scale
        p = jax.nn.softmax(s, axis=-1)
        return jnp.einsum('bkgqs,bskd->bkgqd', p, v)

    o = lax.map(one_block, qb)
    return o.transpose(1, 0, 4, 2, 3, 5).reshape(B, S, H * D)


def lru_combine(e1, e2):
    a1, b1 = e1
    a2, b2 = e2
    return a1 * a2, a2 * b1 + b2


def rg_lru(xc, wa, ba, wx, bx, lam, reverse):
    B, S, W = xc.shape
    xb = xc.reshape(B, S, LRU_BLOCKS, LRU_BLOCK)
    r = jax.nn.sigmoid(jnp.einsum('bsnc,ncd->bsnd', xb, wa).reshape(B, S, W) + ba)
    i = jax.nn.sigmoid(jnp.einsum('bsnc,ncd->bsnd', xb, wx).reshape(B, S, W) + bx)
    log_a = LRU_C * r * jax.nn.log_sigmoid(lam)
    a = jnp.exp(log_a)
    u = jnp.sqrt(-jnp.expm1(2.0 * log_a)) * (i * xc)
    _, h = lax.associative_scan(lru_combine, (a, u), reverse=reverse, axis=1)
    return h


def even_layer(h, w_in, w_out, q_gain, k_gain, conv_w, conv_b, wa, ba, wx, bx, lam):
    B, S, _ = h.shape
    f32 = jnp.float32
    proj = h @ w_in
    q, k, v, g_attn, x_lru, g_lru = split_at(
        proj, [ATTN_WIDTH, KV_WIDTH, KV_WIDTH, ATTN_WIDTH, LRU_WIDTH, LRU_WIDTH])
    q = rmsnorm(q.reshape(B, S, ATTN_HEADS, HEAD_DIM).astype(f32), q_gain)
    k = rmsnorm(k.reshape(B, S, KV_HEADS, HEAD_DIM).astype(f32), k_gain)
    v = v.reshape(B, S, KV_HEADS, HEAD_DIM).astype(f32)
    ang_row, ang_col = axial_angles(S)
    q = axial_rope(q, ang_row, ang_col)
    k = axial_rope(k, ang_row, ang_col)
    attn = block_attention(q, k, v)
    xc = lax.conv_general_dilated(
        x_lru, conv_w[:, None, :], window_strides=(1,), padding=[CONV_PAD],
        dimension_numbers=('NWC', 'WIO', 'NWC'), feature_group_count=LRU_WIDTH) + conv_b
    xc = xc.astype(f32)
    y = (rg_lru(xc, wa[0], ba[0], wx[0], bx[0], lam[0], reverse=False)
         + rg_lru(xc, wa[1], ba[1], wx[1], bx[1], lam[1], reverse=True))
    mix = jnp.concatenate([attn * jax.nn.silu(g_attn.astype(f32)),
                           y * jax.nn.silu(g_lru.astype(f32))], axis=-1)
    return mix.astype(h.dtype) @ w_out


def mlstm_chunkwise(q, k, v, ig, lf):
    B, H, S, dk = q.shape
    dv = v.shape[-1]
    L = MLSTM_CHUNK
    nc = S // L

    def to_chunks(t):
        return jnp.moveaxis(t.reshape((B, H, nc, L) + t.shape[3:]), 2, 0)

    mask = jnp.tril(jnp.ones((L, L), dtype=bool))

    def step(carry, inp):
        C, n, m = carry
        qi, ki, vi, ii, fi = inp
        b = jnp.cumsum(fi, axis=-1)
        logd = jnp.where(mask, b[..., :, None] - b[..., None, :] + ii[..., None, :], -jnp.inf)
        m_inter = b + m[..., None]
        m_t = jnp.maximum(m_inter, jnp.max(logd, axis=-1))
        sc = jnp.einsum('bhld,bhsd->bhls', qi, ki) * jnp.exp(logd - m_t[..., None])
        inter = jnp.exp(m_inter - m_t)
        num = jnp.einsum('bhls,bhsv->bhlv', sc, vi) + inter[..., None] * jnp.einsum('bhld,bhvd->bhlv', qi, C)
        den = jnp.sum(sc, axis=-1) + inter * jnp.einsum('bhld,bhd->bhl', qi, n)
        hout = num / jnp.maximum(jnp.abs(den), jnp.exp(-m_t))[..., None]
        b_last = b[..., -1]
        w = b_last[..., None] - b + ii
        m_new = jnp.maximum(b_last + m, jnp.max(w, axis=-1))
        decay = jnp.exp(b_last + m - m_new)
        ws = jnp.exp(w - m_new[..., None])
        C_new = decay[..., None, None] * C + jnp.einsum('bhsv,bhsd->bhvd', vi * ws[..., None], ki)
        n_new = decay[..., None] * n + jnp.einsum('bhs,bhsd->bhd', ws, ki)
        return (C_new, n_new, m_new), hout

    init = (jnp.zeros((B, H, dv, dk), jnp.float32), jnp.zeros((B, H, dk), jnp.float32),
            jnp.zeros((B, H), jnp.float32))
    _, hc = lax.scan(step, init, (to_chunks(q), to_chunks(k), to_chunks(v), to_chunks(ig), to_chunks(lf)))
    return jnp.moveaxis(hc, 0, 2).reshape(B, H, S, dv)


def odd_layer(h, w_in, gate_bias, norm_gain, w_out):
    B, S, _ = h.shape
    f32 = jnp.float32
    proj = h @ w_in
    q, k, v, o, z, gates = split_at(
        proj, [MLSTM_QK_WIDTH, MLSTM_QK_WIDTH, MLSTM_WIDTH, MLSTM_WIDTH, MLSTM_WIDTH, N_GATE_SETS * MLSTM_HEADS])

    def heads(t, d):
        return t.reshape(B, S, MLSTM_HEADS, d).transpose(0, 2, 1, 3).astype(f32)

    q = heads(q, MLSTM_QK_DIM)
    k = heads(k, MLSTM_QK_DIM) * (MLSTM_QK_DIM ** -0.5)
    v = heads(v, MLSTM_V_DIM)
    g = gates.astype(f32).reshape(B, S, N_GATE_SETS, MLSTM_HEADS) + gate_bias.astype(f32)
    i_f, i_b, f_f, f_b = g.transpose(2, 0, 3, 1)
    h_f = mlstm_chunkwise(q, k, v, i_f, jax.nn.log_sigmoid(f_f))
    flip = lambda t: jnp.flip(t, axis=2)
    h_b = flip(mlstm_chunkwise(flip(q), flip(k), flip(v), flip(i_b), flip(jax.nn.log_sigmoid(f_b))))
    hs = (h_f + h_b).transpose(0, 2, 1, 3)
    hs = jax.nn.sigmoid(o.astype(f32)).reshape(B, S, MLSTM_HEADS, MLSTM_V_DIM) * hs
    hs = rmsnorm(hs, norm_gain.reshape(MLSTM_HEADS, MLSTM_V_DIM))
    hs = hs.reshape(B, S, MLSTM_WIDTH) * jax.nn.silu(z.astype(f32))
    return hs.astype(h.dtype) @ w_out


def setup_inputs(seed: int = 0) -> dict:
    key = jax.random.key(seed)
    ks = jax.random.split(key, 24)
    f32 = jnp.float32
    nrm = lambda k, shape, scale: jax.random.normal(k, shape, f32) * scale
    x = nrm(ks[0], (BATCH, SEQ, D_MODEL), 1.0)
    norm_gain = 1.0 + nrm(ks[1], (DEPTH, D_MODEL), 0.05)
    final_gain = 1.0 + nrm(ks[2], (D_MODEL,), 0.05)
    even_w_in = nrm(ks[3], (N_EVEN, D_MODEL, EVEN_IN), D_MODEL ** -0.5)
    even_w_out = nrm(ks[4], (N_EVEN, EVEN_MIX, D_MODEL), EVEN_MIX ** -0.5)
    q_norm_gain = 1.0 + nrm(ks[5], (N_EVEN, HEAD_DIM), 0.05)
    k_norm_gain = 1.0 + nrm(ks[6], (N_EVEN, HEAD_DIM), 0.05)
    conv_w = nrm(ks[7], (N_EVEN, CONV_W, LRU_WIDTH), CONV_W ** -0.5)
    conv_b = nrm(ks[8], (N_EVEN, LRU_WIDTH), 0.02)
    lru_wa = nrm(ks[9], (N_EVEN, 2, LRU_BLOCKS, LRU_BLOCK, LRU_BLOCK), LRU_BLOCK ** -0.5)
    lru_ba = nrm(ks[10], (N_EVEN, 2, LRU_WIDTH), 0.1)
    lru_wx = nrm(ks[11], (N_EVEN, 2, LRU_BLOCKS, LRU_BLOCK, LRU_BLOCK), LRU_BLOCK ** -0.5)
    lru_bx = nrm(ks[12], (N_EVEN, 2, LRU_WIDTH), 0.1)
    a0 = jax.random.uniform(ks[13], (N_EVEN, 2, LRU_WIDTH), f32, minval=0.9, maxval=0.999)
    p = a0 ** (1.0 / LRU_C)
    lru_lambda = jnp.log(p) - jnp.log1p(-p)
    odd_w_in = nrm(ks[14], (N_ODD, D_MODEL, ODD_IN), D_MODEL ** -0.5)
    i_bias = nrm(ks[15], (N_ODD, 2, MLSTM_HEADS), 0.1)
    f_bias = jnp.broadcast_to(jnp.linspace(3.0, 6.0, MLSTM_HEADS, dtype=f32), (N_ODD, 2, MLSTM_HEADS)) \
        + nrm(ks[16], (N_ODD, 2, MLSTM_HEADS), 0.1)
    odd_gate_bias = jnp.concatenate([i_bias, f_bias], axis=1)
    odd_norm_gain = 1.0 + nrm(ks[17], (N_ODD, MLSTM_WIDTH), 0.05)
    odd_w_out = nrm(ks[18], (N_ODD, MLSTM_WIDTH, D_MODEL), MLSTM_WIDTH ** -0.5)
    return {'x': x, 'norm_gain': norm_gain, 'final_gain': final_gain,
            'even_w_in': even_w_in, 'even_w_out': even_w_out,
            'q_norm_gain': q_norm_gain, 'k_norm_gain': k_norm_gain,
            'conv_w': conv_w, 'conv_b': conv_b,
            'lru_wa': lru_wa, 'lru_ba': lru_ba, 'lru_wx': lru_wx, 'lru_bx': lru_bx,
            'lru_lambda': lru_lambda,
            'odd_w_in': odd_w_in, 'odd_gate_bias': odd_gate_bias,
            'odd_norm_gain': odd_norm_gain, 'odd_w_out': odd_w_out}


def reference(x, norm_gain, final_gain, even_w_in, even_w_out, q_norm_gain, k_norm_gain,
              conv_w, conv_b, lru_wa, lru_ba, lru_wx, lru_bx, lru_lambda,
              odd_w_in, odd_gate_bias, odd_norm_gain, odd_w_out):
    for layer in range(DEPTH):
        hn = rmsnorm(x, norm_gain[layer])
        j = layer // 2
        if layer % 2 == 0:
            out = even_layer(hn, even_w_in[j], even_w_out[j], q_norm_gain[j], k_norm_gain[j],
                             conv_w[j], conv_b[j], lru_wa[j], lru_ba[j], lru_wx[j], lru_bx[j],
                             lru_lambda[j])
        else:
            out = odd_layer(hn, odd_w_in[j], odd_gate_bias[j], odd_norm_gain[j], odd_w_out[j])
        x = x + out.astype(x.dtype)
    return rmsnorm(x, final_gain)
```

```python
import numpy as np
import ml_dtypes
import concourse.bass as bass
import concourse.mybir as mybir
from concourse.bass_utils import run_bass_kernel_spmd

F32 = mybir.dt.float32
BF16 = mybir.dt.bfloat16
AF = mybir.ActivationFunctionType
ALU = mybir.AluOpType
AX = mybir.AxisListType


class Tok:
    __slots__ = ("name", "w", "r")

    def __init__(self, name=""):
        self.name = name
        self.w = None
        self.r = []


class Op:
    __slots__ = ("eng", "fn", "pos", "is_dma", "waits", "dma_waits", "signal", "sigval",
                 "dsem", "dval", "dprev", "known_after", "gid")


ENGS = ("pe", "act", "dve", "pool", "sp")


class Prog:
    def __init__(self, nc, n_dma_sems=12):
        self.nc = nc
        self.ops = {e: [] for e in ENGS}
        self.known = {e: {x: -1 for x in ENGS} for e in ENGS}
        self.dma_seen = {e: set() for e in ENGS}
        self.n_dma_sems = n_dma_sems
        self.dma_count = {e: 0 for e in ENGS}
        self.gid = 0

    def op(self, eng, fn, reads=(), writes=(), dma=False):
        o = Op()
        o.eng = eng
        o.fn = fn
        o.is_dma = dma
        o.pos = len(self.ops[eng])
        o.signal = False
        o.sigval = None
        o.gid = self.gid
        self.gid += 1
        deps = []
        for t in reads:
            if t.w is not None:
                deps.append(t.w)
        for t in writes:
            if t.w is not None:
                deps.append(t.w)
            deps.extend(t.r)
        known = self.known[eng]
        seen = self.dma_seen[eng]
        waits = []
        dma_waits = []
        for d in sorted(set(deps), key=lambda z: -z.gid):
            if d is o:
                continue
            if d.is_dma:
                if d.gid in seen:
                    continue
                seen.add(d.gid)
                dma_waits.append(d)
                continue
            if d.eng == "pe" and eng == "pe":
                continue
            if known[d.eng] >= d.pos:
                continue
            waits.append(d)
            d.signal = True
            for e2, v in d.known_after.items():
                if v > known[e2]:
                    known[e2] = v
        o.waits = waits
        o.dma_waits = dma_waits
        if dma:
            n = self.dma_count[eng]
            self.dma_count[eng] += 1
            o.dsem = n % self.n_dma_sems
            o.dval = 16 * (n // self.n_dma_sems + 1)
            o.dprev = 16 * (n // self.n_dma_sems)
            o.known_after = None
        else:
            ka = dict(known)
            ka[eng] = o.pos
            o.known_after = ka
        for t in reads:
            t.r.append(o)
        for t in writes:
            t.w = o
            t.r = []
        self.ops[eng].append(o)
        return o

    def emit(self):
        nc = self.nc
        import contextlib
        with contextlib.ExitStack() as st:
            csem = {e: st.enter_context(nc.semaphore("cs_" + e)) for e in ENGS if e != "sp"}
            dsem = {e: [st.enter_context(nc.semaphore("ds_%s_%d" % (e, i))) for i in range(self.n_dma_sems)]
                    for e in ENGS if self.dma_count[e] > 0}
            for e in ENGS:
                c = 0
                for o in self.ops[e]:
                    if o.is_dma:
                        continue
                    if o.signal:
                        c += 1
                        o.sigval = c
            final_waits = []
            for e in ENGS:
                for o in self.ops[e]:
                    if o.is_dma:
                        final_waits.append((dsem[e][o.dsem], o.dval))
            block = st.enter_context(nc.Block())

            def run(eng_key, eng):
                for o in self.ops[eng_key]:
                    for d in o.waits:
                        eng.wait_ge(csem[d.eng], d.sigval)
                    for d in o.dma_waits:
                        eng.wait_ge(dsem[d.eng][d.dsem], d.dval)
                    if o.is_dma:
                        if o.dprev > 0:
                            eng.wait_ge(dsem[eng_key][o.dsem], o.dprev)
                        ins = o.fn(eng)
                        ins.then_inc(dsem[eng_key][o.dsem], 16)
                    else:
                        ins = o.fn(eng)
                        if o.signal:
                            ins.then_inc(csem[eng_key], 1)
                if eng_key == "sp":
                    last = {}
                    for s, v in final_waits:
                        k = id(s)
                        if k not in last or last[k][1] < v:
                            last[k] = (s, v)
                    for s, v in last.values():
                        eng.wait_ge(s, v)

            @block.tensor
            def _(eng):
                run("pe", eng)

            @block.scalar
            def _(eng):
                run("act", eng)

            @block.vector
            def _(eng):
                run("dve", eng)

            @block.gpsimd
            def _(eng):
                run("pool", eng)

            @block.sync
            def _(eng):
                run("sp", eng)


def I(method, *args, **kw):
    return lambda e: getattr(e, method)(*args, **kw)


def seq(*fns):
    def f(e):
        r = None
        for g in fns:
            r = g(e)
        return r
    return f


class Buf:
    __slots__ = ("t", "k")

    def __init__(self, t, k):
        self.t = t
        self.k = k


class Arena:
    def __init__(self, nc, st, name, nbytes, gran=2048):
        self.n = nbytes // 2
        self.t = st.enter_context(nc.sbuf_tensor(name, [128, self.n], BF16))
        self.gran = gran
        self.toks = [Tok("%s_%d" % (name, i)) for i in range((nbytes + gran - 1) // gran)]
        self.nbytes = nbytes
        self._priv = []

    def _ap(self, off, nelem, dt, parts):
        sz = 4 if dt == F32 else 2
        nb = nelem * sz
        assert off % 4 == 0 and off + nb <= self.nbytes, (off, nb, self.nbytes)
        ap = self.t[0:parts, off // 2:(off + nb) // 2]
        if dt == F32:
            ap = ap.bitcast(F32)
        return ap, nb

    def private(self, off, nelem, dt, parts=128):
        ap, nb = self._ap(off, nelem, dt, parts)
        t = Tok("priv")
        for g in self.toks[off // self.gran:(off + nb - 1) // self.gran + 1]:
            if g.w is not None:
                t.r.append(g.w)
            t.r.extend(g.r)
        b = Buf(ap, [t])
        self._priv.append((b, off, nb))
        return b

    def release(self):
        for b, off, nb in self._priv:
            for g in self.toks[off // self.gran:(off + nb - 1) // self.gran + 1]:
                for t in b.k:
                    if t.w is not None:
                        g.r.append(t.w)
                    g.r.extend(t.r)
        self._priv = []

    def carve(self, off, nelem, dt, parts=128):
        sz = 4 if dt == F32 else 2
        nb = nelem * sz
        assert off % 4 == 0 and off + nb <= self.nbytes, (off, nb, self.nbytes)
        ap = self.t[0:parts, off // 2:(off + nb) // 2]
        if dt == F32:
            ap = ap.bitcast(F32)
        toks = self.toks[off // self.gran:(off + nb - 1) // self.gran + 1]
        return Buf(ap, list(toks))


EPS = 1e-6
S = 2048
D = 2048
NT = 16
EVEN_IN = 4608
ODD_IN = 8224


class Ctx:
    def __init__(self):
        self.dtoks = {}

    def dtok(self, *key):
        if key not in self.dtoks:
            self.dtoks[key] = Tok(str(key))
        return self.dtoks[key]


def build_program(mode="fused", debug=False, stop=None):
    import contextlib
    nc = bass.Bass("TRN2", target_bir_lowering=False)
    c = Ctx()
    c.nc = nc
    c.debug = debug
    c.dbg = {}
    c.stop = stop
    import os
    if os.environ.get("ML_DIRS"):
        c.ml_dirs = tuple(int(v) for v in os.environ["ML_DIRS"].split(","))
    if os.environ.get("ML_STEPS"):
        c.ml_steps = int(os.environ["ML_STEPS"])

    def din(name, shape, dt=F32):
        return nc.dram_tensor(name, list(shape), dt, kind="ExternalInput").ap()

    def dout(name, shape, dt=F32):
        return nc.dram_tensor(name, list(shape), dt, kind="ExternalOutput").ap()

    def dscr(name, shape, dt=F32):
        if debug:
            ap = nc.dram_tensor(name, list(shape), dt, kind="ExternalOutput").ap()
            c.dbg[name] = ap
            return ap
        return nc.dram_tensor(name, list(shape), dt).ap()

    do0 = mode in ("l0", "fused")
    do1 = mode in ("l1", "fused")
    c.x = din("x", [S, D])
    c.identb = din("identb", [128, 128], BF16)
    c.identf = din("identf", [128, 128])
    if do0:
        c.cosT = din("cosT", [S, 64])
        c.sinT = din("sinT", [S, 64])
        c.gT0 = din("gT0", [128, 16])
        c.w_in0 = din("w_in0", [D, EVEN_IN])
        c.w_out0 = din("w_out0", [D, D])
        c.q_gain = din("q_gain", [1, 128])
        c.k_gain = din("k_gain", [1, 128])
        c.cw = din("cw", [128, 8, 4])
        c.cb = din("cb", [128, 8])
        c.lru_wa = din("lru_wa", [2, 8, 128, 128])
        c.lru_wx = din("lru_wx", [2, 8, 128, 128])
        c.lba = din("lba", [128, 16])
        c.lbx = din("lbx", [128, 16])
        c.llam = din("llam", [128, 16])
        c.gA = dscr("gA", [1024, S])
        c.xl = dscr("xl", [1024, S])
        c.gL = dscr("gL", [1024, S])
    if do1:
        c.gT1 = din("gT1", [128, 16])
        c.w_in1 = din("w_in1", [D, ODD_IN])
        c.w_out1 = din("w_out1", [D, D])
        c.gate_bias = din("gate_bias", [32, 1])
        c.onorm = din("onorm", [1, D])
        c.fgain = din("fgain", [1, D])
        c.mnegf = din("mnegf", [128, 128])
        c.mnegb = din("mnegb", [128, 128])
        for nm, shp, dt_ in (("qT1", [16, 128, 1024], BF16), ("kT1", [16, 128, 1024], BF16), ("k1", [S, 1024], BF16),
                             ("v1", [S, D], BF16), ("osig", [S, D], F32), ("zsil", [S, D], F32),
                             ("gsc", [32, S], F32), ("npm_d", [32, 1024], F32), ("inter_d", [32, 1024], F32),
                             ("dec_d", [1, 256], F32), ("hF", [S, D], F32), ("hB", [S, D], F32)):
            setattr(c, nm, dscr(nm, shp, dt_))
        c.out = dout("out", [S, D])
    if mode == "l0":
        c.x1 = dout("x1", [S, D])
    elif mode == "fused":
        c.x1 = dscr("x1", [S, D])
    else:
        c.x1 = c.x

    p = Prog(nc)
    c.p = p
    with contextlib.ExitStack() as st:
        c.st = st

        def sbt(name, shape, dt):
            t = st.enter_context(nc.sbuf_tensor(name, list(shape), dt))
            return Buf(t[:], [Tok(name)])
        c.sbt = sbt
        c.hm = Arena(nc, st, "hm", 65536, gran=4096)
        c.wa = Arena(nc, st, "wa", 131072, gran=256)
        c.ps = []
        for i in range(8):
            t = st.enter_context(nc.psum_tensor("psb%d" % i, [128, 512], F32))
            c.ps.append(Buf(t[:], [Tok("ps%d" % i)]))
        c.cidb = sbt("c_identb", [128, 128], BF16)
        c.cidf = sbt("c_identf", [128, 128], F32)
        c.onesb = sbt("c_onesb", [128, 128], BF16)
        c.one_col = sbt("c_one", [128, 1], F32)
        c.mone_col = sbt("c_mone", [128, 1], F32)
        c.eps_col = sbt("c_eps", [128, 1], F32)
        p.op("pool", I("memset", c.eps_col.t, EPS), writes=c.eps_col.k)
        p.op("pool", I("memset", c.one_col.t, 1.0), writes=c.one_col.k)
        p.op("pool", I("memset", c.mone_col.t, -1.0), writes=c.mone_col.k)
        p.op("sp", I("dma_start", out=c.cidb.t, in_=c.identb), writes=c.cidb.k, dma=True)
        p.op("sp", I("dma_start", out=c.cidf.t, in_=c.identf), writes=c.cidf.k, dma=True)
        p.op("pool", I("memset", c.onesb.t, 1.0), writes=c.onesb.k)
        if do0:
            emit_layer0(c)
        if do1:
            emit_layer1(c)
        p.emit()
    return nc, c


def r3(ap, inner):
    return ap.rearrange("p (a b) -> p a b", b=inner)


def bc_mid(ap2, n):
    return ap2.unsqueeze(2).broadcast_to([ap2.shape[0], ap2.shape[1], n])


def statcols(c, name, n, parts=128):
    t = c.st.enter_context(c.nc.sbuf_tensor(name, [128, n], F32))
    return [Buf(t[0:parts, j:j + 1], [Tok("%s%d" % (name, j))]) for j in range(n)], t


def phase_A(c, x_src, gT, tag, src_toks=None):
    p, wa = c.p, c.wa
    hmT = c.hmT
    xt = [wa.carve(0, 2048, F32), wa.carve(8192, 2048, F32), wa.carve(65536, 2048, F32)]
    junk = wa.carve(16384, 2048, BF16)
    xs = [wa.carve(20480, 2048, BF16), wa.carve(24576, 2048, BF16)]
    sc, _ = statcols(c, "A%s_st" % tag, 64)
    def ld(i):
        x_ = xt[i % 3]
        p.op("sp", I("dma_start", out=x_.t, in_=x_src[i * 128:(i + 1) * 128, :]),
             reads=(src_toks(i) if src_toks else []), writes=x_.k, dma=True)

    def stage1(i):
        x_ = xt[i % 3]
        ss, rs, rs2, rstd = sc[4 * i:4 * i + 4]
        p.op("act", I("activation", out=junk.t, in_=x_.t, func=AF.Square, accum_out=ss.t),
             reads=x_.k, writes=junk.k + ss.k)
        p.op("act", I("activation", out=rs2.t, in_=ss.t, func=AF.Ln, scale=1.0 / D, bias=c.eps_col.t),
             reads=ss.k + c.eps_col.k, writes=rs2.k)
        p.op("act", I("activation", out=rstd.t, in_=rs2.t, func=AF.Exp, scale=-0.5), reads=rs2.k, writes=rstd.k)

    def stage2(i):
        x_ = xt[i % 3]
        xs_ = xs[i % 2]
        ss, rs, rs2, rstd = sc[4 * i:4 * i + 4]
        p.op("act", I("activation", out=xs_.t, in_=x_.t, func=AF.Copy, scale=rstd.t),
             reads=x_.k + rstd.k, writes=xs_.k)
        for g in range(4):
            pb = c.ps[6 + (g % 2)]
            pT = pb.t.bitcast(BF16)
            p.op("pe", seq(*[I("transpose", out=pT[:, q * 128:(q + 1) * 128],
                               in_=xs_.t[:, (4 * g + q) * 128:(4 * g + q + 1) * 128], identity=c.cidb.t)
                             for q in range(4)]),
                 reads=xs_.k + c.cidb.k, writes=pb.k)
            p.op("dve", I("tensor_tensor", out=hmT[:, 4 * g:4 * g + 4, i * 128:(i + 1) * 128],
                          in0=r3(pT[:, 0:512], 128), in1=bc_mid(gT.t[:, 4 * g:4 * g + 4], 128), op=ALU.mult),
                 reads=pb.k + gT.k, writes=[c.hmk[i]])

    for i in range(3):
        ld(i)
    stage1(0)
    for i in range(NT):
        if i + 1 < NT:
            stage1(i + 1)
        stage2(i)
        if i + 3 < NT:
            ld(i + 3)


def load_w_block(c, wsrc, col0, ncols, wbh):
    p = c.p
    for half in range(2):
        b = wbh[half]
        src = wsrc[half * 1024:(half + 1) * 1024, col0:col0 + ncols].rearrange("(c p) n -> p c n", p=128)
        p.op("pool", I("dma_start", out=r3(b.t, ncols), in_=src), writes=b.k, dma=True)


def wb_views(wbh, ncols):
    v = [r3(wbh[0].t, ncols), r3(wbh[1].t, ncols)]
    return (lambda ch: v[ch // 8][:, ch % 8, :]), wbh[0].k + wbh[1].k


def emit_layer0(c):
    nc, p, wa = c.nc, c.p, c.wa
    c.hmT = r3(c.hm.t[:, :], S)
    c.hmk = [Tok("hm%d" % i) for i in range(NT)]
    hmT, hmk = c.hmT, c.hmk
    sbt = c.sbt
    cos = wa.carve(114688, 16 * 64, F32)
    sin = wa.carve(118784, 16 * 64, F32)
    cos.t = r3(cos.t, 64)
    sin.t = r3(sin.t, 64)
    gT0 = sbt("gT0s", [128, 16], F32)
    qgb = sbt("qgb", [128, 512], F32)
    kgb = sbt("kgb", [128, 256], F32)
    p.op("sp", I("dma_start", out=cos.t, in_=c.cosT.rearrange("(i p) f -> p i f", p=128)), writes=cos.k, dma=True)
    p.op("sp", I("dma_start", out=sin.t, in_=c.sinT.rearrange("(i p) f -> p i f", p=128)), writes=sin.k, dma=True)
    p.op("sp", I("dma_start", out=gT0.t, in_=c.gT0), writes=gT0.k, dma=True)
    p.op("sp", I("dma_start", out=r3(qgb.t, 128), in_=bass.AP(c.q_gain.tensor, 0, [[0, 128], [0, 4], [1, 128]])),
         writes=qgb.k, dma=True)
    p.op("sp", I("dma_start", out=r3(kgb.t, 128), in_=bass.AP(c.k_gain.tensor, 0, [[0, 128], [0, 2], [1, 128]])),
         writes=kgb.k, dma=True)

    phase_A(c, c.x, gT0, "0")

    WB = [[wa.carve(32768, 8 * 512, BF16), wa.carve(32768 + 8192, 8 * 512, BF16)],
          [wa.carve(49152, 8 * 512, BF16), wa.carve(49152 + 8192, 8 * 512, BF16)]]
    QT_OFF, KT_OFF, V_OFF = 65536, 98304, 106496
    qT = wa.carve(QT_OFF, 8 * S, BF16)
    kT = wa.carve(KT_OFF, 2 * S, BF16)
    vS = wa.carve(V_OFF, 16 * 256, BF16)
    qT3, kT3, vS3 = r3(qT.t, S), r3(kT.t, S), r3(vS.t, 256)

    def tk(base, h0, h1, t0, t1):
        out = []
        for h in range(h0, h1):
            o = base + h * 4096 + t0 * 2
            for tkk in wa.toks[o // wa.gran:(o + (t1 - t0) * 2 - 1) // wa.gran + 1]:
                if tkk not in out:
                    out.append(tkk)
        return out

    sets = []
    NSET = 3
    for s_ in range(NSET):
        b0 = s_ * 9216
        sqn = wa.carve(b0, 512, F32)
        d = dict(sq=sqn, qn=sqn, qg=wa.carve(b0 + 2048, 512, F32),
                 t1=wa.carve(b0 + 4096, 256, F32), t2=wa.carve(b0 + 5120, 256, F32),
                 t3=wa.carve(b0 + 6144, 256, F32), t4=wa.carve(b0 + 7168, 256, F32),
                 qr=wa.carve(b0 + 8192, 512, BF16))
        sets.append(d)
    _, b1t = statcols(c, "B1st", 16 * NSET)
    b1ring = [[Buf(b1t[:, s_ * 16 + 4 * q:s_ * 16 + 4 * q + 4], [Tok("b1s")]) for q in range(4)] for s_ in range(NSET)]

    def stat4(nh, sidx):
        return [Buf(b.t[:, 0:nh], b.k) for b in b1ring[sidx]]

    def process_qk(psb, col0, nh, gain, dstT3, dst_base, head0, i, sidx):
        w = sets[sidx]
        n = nh * 128
        psv = psb.t[:, col0:col0 + n]
        ss, rs, rs2, rstd = stat4(nh, sidx)
        sq, qn, qg, qr = w["sq"], w["qn"], w["qg"], w["qr"]
        p.op("act", I("activation", out=sq.t[:, 0:n], in_=psv, func=AF.Square), reads=psb.k, writes=sq.k)
        p.op("dve", I("tensor_reduce", out=ss.t, in_=r3(sq.t[:, 0:n], 128), axis=AX.X, op=ALU.add),
             reads=sq.k, writes=ss.k)
        p.op("dve", I("tensor_scalar", out=rs.t, in0=ss.t, scalar1=1.0 / 128, scalar2=EPS, op0=ALU.mult, op1=ALU.add),
             reads=ss.k, writes=rs.k)
        p.op("act", I("activation", out=rs2.t, in_=rs.t, func=AF.Sqrt), reads=rs.k, writes=rs2.k)
        p.op("dve", I("reciprocal", out=rstd.t, in_=rs2.t), reads=rs2.k, writes=rstd.k)
        p.op("dve", I("tensor_tensor", out=r3(qn.t[:, 0:n], 128), in0=r3(psv, 128), in1=bc_mid(rstd.t, 128), op=ALU.mult),
             reads=psb.k + rstd.k, writes=qn.k)
        p.op("dve", I("tensor_tensor", out=qg.t[:, 0:n], in0=qn.t[:, 0:n], in1=gain.t[:, 0:n], op=ALU.mult),
             reads=qn.k + gain.k, writes=qg.k)
        qg5 = qg.t[:, 0:n].rearrange("p (h g f e) -> p h g f e", g=2, f=2, e=32)
        qr5 = qr.t[:, 0:n].rearrange("p (h g f e) -> p h g f e", g=2, f=2, e=32)
        X1, X2 = qg5[:, :, :, 0, :], qg5[:, :, :, 1, :]
        Cb = cos.t[:, i, :].rearrange("p (g e) -> p g e", e=32).unsqueeze(1).broadcast_to([128, nh, 2, 32])
        Sb = sin.t[:, i, :].rearrange("p (g e) -> p g e", e=32).unsqueeze(1).broadcast_to([128, nh, 2, 32])

        def tv(b):
            return b.t[:, 0:nh * 64].rearrange("p (h g e) -> p h g e", g=2, e=32)
        t1, t2, t3, t4 = w["t1"], w["t2"], w["t3"], w["t4"]
        p.op("dve", I("tensor_tensor", out=tv(t1), in0=X1, in1=Cb, op=ALU.mult), reads=qg.k + cos.k, writes=t1.k)
        p.op("dve", I("tensor_tensor", out=tv(t2), in0=X2, in1=Sb, op=ALU.mult), reads=qg.k + sin.k, writes=t2.k)
        p.op("pool", I("tensor_tensor", out=tv(t3), in0=X1, in1=Sb, op=ALU.mult), reads=qg.k + sin.k, writes=t3.k)
        p.op("dve", I("tensor_tensor", out=tv(t4), in0=X2, in1=Cb, op=ALU.mult), reads=qg.k + cos.k, writes=t4.k)
        p.op("dve", I("tensor_tensor", out=qr5[:, :, :, 0, :], in0=tv(t1), in1=tv(t2), op=ALU.subtract),
             reads=t1.k + t2.k, writes=qr.k)
        p.op("dve", I("tensor_tensor", out=qr5[:, :, :, 1, :], in0=tv(t3), in1=tv(t4), op=ALU.add),
             reads=t3.k + t4.k, writes=qr.k)
        def part2():
            pb = c.ps[6 + (sidx % 2)]
            pT = pb.t.bitcast(BF16)
            p.op("pe", seq(*[I("transpose", out=pT[:, h * 128:(h + 1) * 128], in_=qr.t[:, h * 128:(h + 1) * 128],
                               identity=c.cidb.t) for h in range(nh)]),
                 reads=qr.k + c.cidb.k, writes=pb.k)
            p.op("act", I("activation", out=dstT3[:, head0:head0 + nh, i * 128:(i + 1) * 128], in_=r3(pT[:, 0:n], 128),
                          func=AF.Copy),
                 reads=pb.k, writes=tk(dst_base, head0, head0 + nh, i * 128, (i + 1) * 128))
        return part2

    cnt = 0
    pend = [None]
    for nb in range(3):
        wbh = WB[nb % 2]
        if nb == 0:
            load_w_block(c, c.w_in0, 0, 512, WB[0])
        load_w_block(c, c.w_in0, (nb + 1) * 512, 512, WB[(nb + 1) % 2])
        wv, wk = wb_views(wbh, 512)
        for i in range(NT):
            psb = c.ps[cnt % 2]
            p.op("pe", seq(*[I("matmul", psb.t, lhsT=hmT[:, ch, i * 128:(i + 1) * 128], rhs=wv(ch),
                               start=(ch == 0), stop=(ch == 15)) for ch in range(16)]),
                 reads=[hmk[i]] + wk, writes=psb.k)
            if pend[0] is not None:
                pend[0]()
            if nb < 2:
                pend[0] = process_qk(psb, 0, 4, qgb, qT3, QT_OFF, 4 * nb, i, cnt % NSET)
            else:
                pend[0] = process_qk(psb, 0, 2, kgb, kT3, KT_OFF, 0, i, cnt % NSET)
                vo = V_OFF + i * 512
                p.op("act", I("activation", out=vS3[:, i, :], in_=psb.t[:, 256:512], func=AF.Copy),
                     reads=psb.k, writes=wa.toks[vo // wa.gran:(vo + 511) // wa.gran + 1])
            cnt += 1

    pend[0]()
    stg = [wa.carve(0, 2048, F32), wa.carve(8192, 2048, F32)]
    dsts = [c.gA, c.xl, c.gL]
    pcnt = 0
    for jb in range(6):
        wbh = WB[(3 + jb) % 2]
        if jb > 0:
            load_w_block(c, c.w_in0, 1536 + jb * 512, 512, wbh)
        wv, wk = wb_views(wbh, 512)
        for q4 in range(4):
            j = jb * 4 + q4
            kind = j // 8
            sg = stg[j % 2]
            for tb in range(4):
                psb = c.ps[2 + (pcnt % 4)]
                pcnt += 1
                p.op("pe", seq(*[I("matmul", psb.t, lhsT=wv(ch)[:, q4 * 128:(q4 + 1) * 128],
                                   rhs=hmT[:, ch, tb * 512:(tb + 1) * 512], start=(ch == 0), stop=(ch == 15))
                                 for ch in range(16)]),
                     reads=hmk[4 * tb:4 * tb + 4] + wk, writes=psb.k)
                p.op("act", I("activation", out=sg.t[:, tb * 512:(tb + 1) * 512], in_=psb.t,
                              func=(AF.Copy if kind == 1 else AF.Silu)),
                     reads=psb.k, writes=sg.k)
            p.op("sp", I("dma_start", out=dsts[kind][(j % 8) * 128:(j % 8 + 1) * 128, :], in_=sg.t),
                 reads=sg.k, writes=[c.dtok(kind, j % 8)], dma=True)

    emit_attention(c, qT3, kT3, vS3, QT_OFF, KT_OFF, V_OFF, tk)
    emit_lru(c)
    emit_outproj(c, c.w_out0, c.x, c.x1, WB, final_gain=None)


def emit_attention(c, qT3, kT3, vS3, QT_OFF, KT_OFF, V_OFF, tk):
    p, wa = c.p, c.wa
    hmT, hmk = c.hmT, c.hmk
    SCALE = 128 ** -0.5
    gAb = [wa.carve(0, 2048, F32), wa.carve(8192, 2048, F32)]
    NP = 4
    PT = [wa.carve(16384 + 1024 * q, 512, BF16) for q in range(NP)]
    rec = [wa.carve(20480 + 2048 * q, 512, F32) for q in range(2)]
    ob = [wa.carve(24576 + 2048 * q, 512, F32) for q in range(2)]
    sT = [c.ps[0], c.ps[1], c.ps[6], c.ps[7]]
    psO = [c.ps[2], c.ps[3]]
    psS = [c.ps[4], c.ps[5]]
    vtoks = wa.toks[V_OFF // wa.gran:(V_OFF + 8191) // wa.gran + 1]
    blk = 0
    ptc = 0
    for h in range(8):
        kvh = h // 4
        g_ = gAb[h % 2]
        p.op("sp", I("dma_start", out=g_.t, in_=c.gA[h * 128:(h + 1) * 128, :]),
             reads=[c.dtok(0, h)], writes=g_.k, dma=True)
        for qb in range(4):
            O, SM = psO[blk % 2], psS[blk % 2]
            qtok = tk(QT_OFF, h, h + 1, qb * 512, (qb + 1) * 512)
            ktok = tk(KT_OFF, kvh, kvh + 1, 0, S)

            def emit_S(st):
                b = sT[st % 4]
                p.op("pe", I("matmul", b.t, lhsT=kT3[:, kvh, st * 128:(st + 1) * 128],
                             rhs=qT3[:, h, qb * 512:(qb + 1) * 512], start=True, stop=True),
                     reads=qtok + ktok, writes=b.k)

            def emit_E(st, pt):
                b = sT[st % 4]
                p.op("act", I("activation", out=pt.t, in_=b.t, func=AF.Exp, scale=SCALE), reads=b.k, writes=pt.k)

            def emit_PV(st, pt):
                p.op("pe", seq(I("matmul", O.t, lhsT=vS3[:, st, kvh * 128:(kvh + 1) * 128], rhs=pt.t,
                                 start=(st == 0), stop=(st == 15)),
                               I("matmul", SM.t, lhsT=c.onesb.t, rhs=pt.t, start=(st == 0), stop=(st == 15))),
                     reads=pt.k + vtoks + c.onesb.k, writes=O.k + SM.k)
            pts = {}
            emit_S(0)
            emit_S(1)
            emit_S(2)
            for st in range(16):
                pts[st] = PT[ptc % NP]
                ptc += 1
                emit_E(st, pts[st])
                if st + 3 < 16:
                    emit_S(st + 3)
                emit_PV(st, pts[st])
            r_, o_ = rec[blk % 2], ob[blk % 2]
            p.op("dve", I("reciprocal", out=r_.t, in_=SM.t), reads=SM.k, writes=r_.k)
            p.op("dve", I("tensor_tensor", out=o_.t, in0=O.t, in1=r_.t, op=ALU.mult), reads=O.k + r_.k, writes=o_.k)
            p.op("pool", I("tensor_tensor", out=hmT[:, h, qb * 512:(qb + 1) * 512], in0=o_.t,
                           in1=g_.t[:, qb * 512:(qb + 1) * 512], op=ALU.mult),
                 reads=o_.k + g_.k, writes=hmk[4 * qb:4 * qb + 4])
            blk += 1


def emit_lru(c):
    p, wa = c.p, c.wa
    hmT, hmk = c.hmT, c.hmk
    sbt = c.sbt
    cw = sbt("cw_s", [128, 8, 4], F32)
    cb = sbt("cb_s", [128, 8], F32)
    lba = sbt("lba_s", [128, 16], F32)
    lbx = sbt("lbx_s", [128, 16], F32)
    lam = sbt("lam_s", [128, 16], F32)
    for b, src in ((cw, c.cw), (cb, c.cb), (lba, c.lba), (lbx, c.lbx), (lam, c.llam)):
        p.op("sp", I("dma_start", out=b.t, in_=src), writes=b.k, dma=True)
    wa_sb = wa.carve(118784, 16 * 128, BF16)
    wx_sb = wa.carve(122880, 16 * 128, BF16)
    p.op("pool", I("dma_start", out=r3(wa_sb.t, 128), in_=c.lru_wa.rearrange("r n c d -> c (r n) d")),
         writes=wa_sb.k, dma=True)
    p.op("pool", I("dma_start", out=r3(wx_sb.t, 128), in_=c.lru_wx.rearrange("r n c d -> c (r n) d")),
         writes=wx_sb.k, dma=True)
    wa3, wx3 = r3(wa_sb.t, 128), r3(wx_sb.t, 128)
    e_ = sbt("l_e", [128, 16], F32)
    l_ = sbt("l_l", [128, 16], F32)
    u_ = sbt("l_u", [128, 16], F32)
    m_ = sbt("l_m", [128, 16], F32)
    sc4 = sbt("l_sc4", [128, 16], F32)
    p.op("act", I("activation", out=e_.t, in_=lam.t, func=AF.Exp, scale=-1.0), reads=lam.k, writes=e_.k)
    p.op("act", I("activation", out=l_.t, in_=e_.t, func=AF.Ln, bias=1.0), reads=e_.k, writes=l_.k)
    p.op("dve", I("tensor_scalar", out=u_.t, in0=e_.t, scalar1=-1.0 / 3, scalar2=0.5, op0=ALU.mult, op1=ALU.add),
         reads=e_.k, writes=u_.k)
    p.op("dve", I("tensor_tensor", out=u_.t, in0=u_.t, in1=e_.t, op=ALU.mult), reads=u_.k + e_.k, writes=u_.k)
    p.op("dve", I("tensor_scalar", out=u_.t, in0=u_.t, scalar1=-1.0, scalar2=1.0, op0=ALU.mult, op1=ALU.add),
         reads=u_.k, writes=u_.k)
    p.op("dve", I("tensor_tensor", out=u_.t, in0=u_.t, in1=e_.t, op=ALU.mult), reads=u_.k + e_.k, writes=u_.k)
    p.op("dve", I("tensor_single_scalar", out=m_.t, in_=e_.t, scalar=0.05, op=ALU.is_lt), reads=e_.k, writes=m_.k)
    p.op("dve", I("tensor_tensor", out=u_.t, in0=u_.t, in1=l_.t, op=ALU.subtract), reads=u_.k + l_.k, writes=u_.k)
    p.op("dve", I("tensor_tensor", out=u_.t, in0=u_.t, in1=m_.t, op=ALU.mult), reads=u_.k + m_.k, writes=u_.k)
    p.op("dve", I("tensor_tensor", out=u_.t, in0=u_.t, in1=l_.t, op=ALU.add), reads=u_.k + l_.k, writes=u_.k)
    p.op("dve", I("tensor_scalar", out=sc4.t, in0=u_.t, scalar1=4.0, scalar2=None, op0=ALU.mult), reads=u_.k, writes=sc4.k)

    p2 = sbt("l_p2", [128, 16], F32)
    n2 = sbt("l_n2", [128, 16], F32)
    p.op("dve", I("tensor_scalar", out=p2.t, in0=sc4.t, scalar1=2.0, scalar2=None, op0=ALU.mult), reads=sc4.k, writes=p2.k)
    p.op("dve", I("tensor_scalar", out=n2.t, in0=sc4.t, scalar1=-2.0, scalar2=None, op0=ALU.mult), reads=sc4.k, writes=n2.k)
    n4 = sbt("l_n4", [128, 16], F32)
    p.op("dve", I("tensor_scalar", out=n4.t, in0=sc4.t, scalar1=-4.0, scalar2=None, op0=ALU.mult), reads=sc4.k, writes=n4.k)
    K8 = 8192
    X = [wa.carve(0, 2048, F32), wa.carve(K8, 2048, F32)]
    GL = wa.carve(2 * K8, 2048, F32)
    XC = wa.carve(3 * K8, 2048, F32)
    Rb = [wa.carve(4 * K8, 2048, F32), wa.carve(5 * K8, 2048, F32)]
    Ab = [wa.carve(6 * K8, 2048, F32), wa.carve(7 * K8, 2048, F32)]
    Pb = [wa.carve(8 * K8, 2048, F32), wa.carve(9 * K8, 2048, F32)]
    Ibb = [wa.carve(10 * K8, 2048, F32), wa.carve(11 * K8, 2048, F32)]
    H = wa.carve(12 * K8, 2048, F32)
    Y = wa.carve(13 * K8, 2048, F32)
    XCB = wa.carve(14 * K8, 2048, BF16)
    pc = 0

    def load(j):
        p.op("sp", I("dma_start", out=X[j % 2].t, in_=c.xl[j * 128:(j + 1) * 128, :]), reads=[c.dtok(1, j)],
             writes=X[j % 2].k, dma=True)
    pcn = [0]

    def gate_mm(j, dr, w3, bias, dst):
        gi = dr * 8 + j
        for tb in range(4):
            psb = c.ps[pcn[0] % 6]
            pcn[0] += 1
            p.op("pe", I("matmul", psb.t, lhsT=w3[:, gi, :], rhs=XCB.t[:, tb * 512:(tb + 1) * 512],
                         start=True, stop=True), reads=XCB.k + wa_sb.k + wx_sb.k, writes=psb.k)
            p.op("act", I("activation", out=dst.t[:, tb * 512:(tb + 1) * 512], in_=psb.t, func=AF.Sigmoid,
                          bias=bias.t[:, gi:gi + 1]), reads=psb.k + bias.k, writes=dst.k)

    def head(j):
        x_ = X[j % 2]
        if j + 1 < 8:
            load(j + 1)
        p.op("dve", I("tensor_scalar", out=XC.t, in0=x_.t, scalar1=cw.t[:, j, 2:3], scalar2=cb.t[:, j:j + 1],
                      op0=ALU.mult, op1=ALU.add), reads=x_.k + cw.k + cb.k, writes=XC.k)
        p.op("dve", I("scalar_tensor_tensor", out=XC.t[:, 2:S], in0=x_.t[:, 0:S - 2], scalar=cw.t[:, j, 0:1],
                      in1=XC.t[:, 2:S], op0=ALU.mult, op1=ALU.add), reads=x_.k + XC.k, writes=XC.k)
        p.op("dve", I("scalar_tensor_tensor", out=XC.t[:, 1:S], in0=x_.t[:, 0:S - 1], scalar=cw.t[:, j, 1:2],
                      in1=XC.t[:, 1:S], op0=ALU.mult, op1=ALU.add), reads=x_.k + XC.k, writes=XC.k)
        p.op("dve", I("scalar_tensor_tensor", out=XC.t[:, 0:S - 1], in0=x_.t[:, 1:S], scalar=cw.t[:, j, 3:4],
                      in1=XC.t[:, 0:S - 1], op0=ALU.mult, op1=ALU.add), reads=x_.k + XC.k, writes=XC.k)
        p.op("act", I("activation", out=XCB.t, in_=XC.t, func=AF.Copy), reads=XC.k, writes=XCB.k)
        for dr in range(2):
            gate_mm(j, dr, wa3, lba, Rb[dr])

    def mid(j):
        p.op("sp", I("dma_start", out=GL.t, in_=c.gL[j * 128:(j + 1) * 128, :]), reads=[c.dtok(2, j)], writes=GL.k, dma=True)
        for dr in range(2):
            gate_mm(j, dr, wx3, lbx, Ibb[dr])
        for dr in range(2):
            gi = dr * 8 + j
            R, A, P_, Ib = Rb[dr], Ab[dr], Pb[dr], Ibb[dr]
            p.op("act", I("activation", out=A.t, in_=R.t, func=AF.Exp, scale=n2.t[:, gi:gi + 1]), reads=R.k + n2.k, writes=A.k)
            p.op("act", I("activation", out=P_.t, in_=R.t, func=AF.Exp, scale=n4.t[:, gi:gi + 1]), reads=R.k + n4.k, writes=P_.k)
            p.op("act", I("activation", out=R.t, in_=R.t, func=AF.Tanh, scale=p2.t[:, gi:gi + 1]), reads=R.k + p2.k, writes=R.k)
            p.op("pool", I("tensor_tensor", out=Ib.t, in0=Ib.t, in1=XC.t, op=ALU.mult), reads=Ib.k + XC.k, writes=Ib.k)
            p.op("dve", I("scalar_tensor_tensor", out=P_.t, in0=P_.t, scalar=1.0, in1=R.t, op0=ALU.add, op1=ALU.mult),
                 reads=P_.k + R.k, writes=P_.k)

    def tailA(j):
        for dr in range(2):
            P_ = Pb[dr]
            p.op("act", I("activation", out=P_.t, in_=P_.t, func=AF.Sqrt), reads=P_.k, writes=P_.k)

    def tail(j):
        for dr in range(2):
            R, A, P_, Ib = Rb[dr], Ab[dr], Pb[dr], Ibb[dr]
            p.op("dve", I("tensor_tensor", out=Ib.t, in0=Ib.t, in1=P_.t, op=ALU.mult), reads=Ib.k + P_.k, writes=Ib.k)
            if dr == 0:
                p.op("dve", I("tensor_tensor_scan", out=Y.t, data0=A.t, data1=Ib.t, initial=0.0,
                              op0=ALU.mult, op1=ALU.add), reads=A.k + Ib.k, writes=Y.k)
            else:
                p.op("dve", I("tensor_tensor_scan", out=H.t[:, ::-1], data0=A.t[:, ::-1], data1=Ib.t[:, ::-1],
                              initial=0.0, op0=ALU.mult, op1=ALU.add), reads=A.k + Ib.k, writes=H.k)
        p.op("pool", I("tensor_tensor", out=Y.t, in0=Y.t, in1=H.t, op=ALU.add), reads=Y.k + H.k, writes=Y.k)
        p.op("pool", I("tensor_tensor", out=hmT[:, 8 + j, :], in0=Y.t, in1=GL.t, op=ALU.mult),
             reads=Y.k + GL.k, writes=hmk)

    load(0)
    head(0)
    for j in range(8):
        mid(j)
        tailA(j)
        if j + 1 < 8:
            head(j + 1)
        tail(j)


def emit_outproj(c, w_out, x_src, x_dst, WB, final_gain=None):
    p, wa = c.p, c.wa
    hmT, hmk = c.hmT, c.hmk
    NB_ = 4
    xin = [wa.carve(2048 * q, 512, F32) for q in range(NB_)]
    xo = [wa.carve(8192 + 2048 * q, 512, F32) for q in range(NB_)]
    jobs = [(nb, i) for nb in range(4) for i in range(NT)]

    def load(k):
        nb, i = jobs[k]
        xi = xin[k % NB_]
        p.op("sp", I("dma_start", out=xi.t, in_=x_src[i * 128:(i + 1) * 128, nb * 512:(nb + 1) * 512]),
             writes=xi.k, dma=True)
    for k in range(min(3, len(jobs))):
        load(k)
    wv = wk = None
    for k, (nb, i) in enumerate(jobs):
        if i == 0:
            if nb == 0:
                load_w_block(c, w_out, 0, 512, WB[0])
            if nb + 1 < 4:
                load_w_block(c, w_out, (nb + 1) * 512, 512, WB[(nb + 1) % 2])
            wv, wk = wb_views(WB[nb % 2], 512)
        if k + 3 < len(jobs):
            load(k + 3)
        psb = c.ps[k % 4]
        xi, xo_ = xin[k % NB_], xo[k % NB_]
        p.op("pe", seq(*[I("matmul", psb.t, lhsT=hmT[:, ch, i * 128:(i + 1) * 128], rhs=wv(ch),
                           start=(ch == 0), stop=(ch == 15)) for ch in range(16)]),
             reads=[hmk[i]] + wk, writes=psb.k)
        p.op("dve", I("tensor_tensor", out=xo_.t, in0=psb.t, in1=xi.t, op=ALU.add),
             reads=psb.k + xi.k, writes=xo_.k)
        p.op("pool", I("dma_start", out=x_dst[i * 128:(i + 1) * 128, nb * 512:(nb + 1) * 512], in_=xo_.t),
             reads=xo_.k, writes=[c.dtok("x1", i, nb)], dma=True)


def host_consts():
    inv = (10000.0 ** (-np.arange(0, 64, 2, dtype=np.float32) / 64)).astype(np.float32)
    t = np.arange(S)
    row = (t // 64).astype(np.float32)
    col = (t % 64).astype(np.float32)
    ar = row[:, None] * inv[None, :]
    ac = col[:, None] * inv[None, :]
    cosT = np.concatenate([np.cos(ar), np.cos(ac)], axis=1).astype(np.float32)
    sinT = np.concatenate([np.sin(ar), np.sin(ac)], axis=1).astype(np.float32)
    identb = np.eye(128, dtype=np.float32).astype(ml_dtypes.bfloat16)
    identf = np.eye(128, dtype=np.float32)
    sg, ta = np.meshgrid(np.arange(128), np.arange(128), indexing="ij")
    mneg = np.where(sg <= ta, 0.0, -1e30).astype(np.float32)
    return dict(cosT=cosT, sinT=sinT, identb=identb, identf=identf, mneg=mneg)


def shared_inputs(inp, mode):
    f = np.ascontiguousarray
    hc = host_consts()
    m = dict(identb=hc["identb"], identf=hc["identf"])
    if mode in ("l0", "fused"):
        m.update(cosT=hc["cosT"], sinT=hc["sinT"],
                 gT0=f(inp["norm_gain"][0].reshape(16, 128).T),
                 w_in0=f(inp["even_w_in"][0]), w_out0=f(inp["even_w_out"][0]),
                 q_gain=f(inp["q_norm_gain"][0].reshape(1, 128)), k_gain=f(inp["k_norm_gain"][0].reshape(1, 128)),
                 cw=f(inp["conv_w"][0].reshape(4, 8, 128).transpose(2, 1, 0)),
                 cb=f(inp["conv_b"][0].reshape(8, 128).T),
                 lru_wa=f(inp["lru_wa"][0]), lru_wx=f(inp["lru_wx"][0]),
                 lba=f(inp["lru_ba"][0].reshape(2, 8, 128).transpose(2, 0, 1).reshape(128, 16)),
                 lbx=f(inp["lru_bx"][0].reshape(2, 8, 128).transpose(2, 0, 1).reshape(128, 16)),
                 llam=f(inp["lru_lambda"][0].reshape(2, 8, 128).transpose(2, 0, 1).reshape(128, 16)))
    if mode in ("l1", "fused"):
        m.update(gT1=f(inp["norm_gain"][1].reshape(16, 128).T),
                 w_in1=f(inp["odd_w_in"][0]), w_out1=f(inp["odd_w_out"][0]),
                 gate_bias=f(inp["odd_gate_bias"][0].reshape(32, 1)),
                 onorm=f(inp["odd_norm_gain"][0].reshape(1, D)),
                 fgain=f(inp["final_gain"].reshape(1, D)),
                 mnegf=hc["mneg"], mnegb=np.ascontiguousarray(hc["mneg"].T))
    return m


_CACHE = {}


def run_mode(mode, inp, x_full, debug=False):
    key = (mode, debug)
    if key not in _CACHE:
        _CACHE[key] = build_program(mode, debug)
    nc, c = _CACHE[key]
    sh = shared_inputs(inp, mode)
    in_maps = []
    for b in range(8):
        m = dict(sh)
        m["x"] = np.ascontiguousarray(x_full[b])
        in_maps.append(m)
    res = run_bass_kernel_spmd(nc, in_maps, core_ids=list(range(8)))
    return res.results


def kernel(**inputs):
    inp = {k: np.asarray(v) for k, v in inputs.items()}
    x = inp["x"].astype(np.float32, copy=False)
    r = run_mode("fused", inp, x)
    return np.stack([r[b]["out"] for b in range(8)], axis=0).astype(np.float32)


def emit_layer1(c):
    nc, p, wa = c.nc, c.p, c.wa
    if not hasattr(c, "hmT"):
        c.hmT = r3(c.hm.t[:, :], S)
        c.hmk = [Tok("hm%d" % i) for i in range(NT)]
    hmT, hmk = c.hmT, c.hmk
    sbt = c.sbt
    gT1 = sbt("gT1s", [128, 16], F32)
    gbias = sbt("gbias", [32, 1], F32)
    mnf = sbt("mnf", [128, 128], F32)
    mnb = sbt("mnb", [128, 128], F32)
    p.op("sp", I("dma_start", out=gT1.t, in_=c.gT1), writes=gT1.k, dma=True)
    p.op("sp", I("dma_start", out=gbias.t, in_=c.gate_bias), writes=gbias.k, dma=True)
    p.op("sp", I("dma_start", out=mnf.t, in_=c.mnegf), writes=mnf.k, dma=True)
    p.op("sp", I("dma_start", out=mnb.t, in_=c.mnegb), writes=mnb.k, dma=True)

    fused = ("x1", 0, 0) in c.dtoks
    phase_A(c, c.x1, gT1, "1", src_toks=(lambda i: [c.dtok("x1", i, nb) for nb in range(4)]) if fused else None)

    WB = [[wa.carve(32768, 8 * 512, BF16), wa.carve(32768 + 8192, 8 * 512, BF16)],
          [wa.carve(49152, 8 * 512, BF16), wa.carve(49152 + 8192, 8 * 512, BF16)]]
    wbi = 0
    wg = wa.carve(16384, 16 * 32, BF16)
    p.op("pool", I("dma_start", out=r3(wg.t, 32), in_=c.w_in1[:, 8192:8224].rearrange("(c p) n -> p c n", p=128)),
         writes=wg.k, dma=True)
    wg3 = r3(wg.t, 32)
    Gt = wa.carve(20480, 2048, F32, parts=32)
    for tb in range(4):
        psb = c.ps[2 + tb]
        p.op("pe", seq(*[I("matmul", psb.t[0:32, :], lhsT=wg3[:, ch, :], rhs=hmT[:, ch, tb * 512:(tb + 1) * 512],
                           start=(ch == 0), stop=(ch == 15)) for ch in range(16)]),
             reads=hmk[4 * tb:4 * tb + 4] + wg.k, writes=psb.k)
        p.op("act", I("activation", out=Gt.t[:, tb * 512:(tb + 1) * 512], in_=psb.t[0:32, :], func=AF.Identity,
                      bias=gbias.t), reads=psb.k + gbias.k, writes=Gt.k)
    p.op("sp", I("dma_start", out=c.gsc, in_=Gt.t), reads=Gt.k, writes=[c.dtok("gsc")], dma=True)

    gates_gen = emit_gates(c)

    def gstep(n=1):
        for _ in range(n):
            next(gates_gen, None)

    stgT = [wa.carve(0, 2048, BF16), wa.carve(4096, 2048, BF16)]
    pc = 0
    for jb in range(4):
        wbh = WB[wbi % 2]
        wbi += 1
        load_w_block(c, c.w_in1, jb * 512, 512, wbh)
        wv, wk = wb_views(wbh, 512)
        for q4 in range(4):
            j = jb * 4 + q4
            sg = stgT[j % 2]
            dst = c.qT1 if j < 8 else c.kT1
            scl = 1.0 if j < 8 else 128 ** -0.5
            for tb in range(4):
                psb = c.ps[2 + (pc % 4)]
                pc += 1
                p.op("pe", seq(*[I("matmul", psb.t, lhsT=wv(ch)[:, q4 * 128:(q4 + 1) * 128],
                                   rhs=hmT[:, ch, tb * 512:(tb + 1) * 512], start=(ch == 0), stop=(ch == 15))
                                 for ch in range(16)]),
                     reads=hmk[4 * tb:4 * tb + 4] + wk, writes=psb.k)
                p.op("act", I("activation", out=sg.t[:, tb * 512:(tb + 1) * 512], in_=psb.t, func=AF.Copy, scale=scl),
                     reads=psb.k, writes=sg.k)
                gstep(1)
            p.op("sp", I("dma_start", out=bass.AP(dst.tensor, (j % 8) * 128, [[1024, 128], [128 * 1024, 16], [1, 128]]),
                         in_=r3(sg.t, 128)), reads=sg.k, writes=[c.dtok("qk", j)], dma=True)
    onb1 = wa.carve(24576, 2048, F32)
    p.op("sp", I("dma_start", out=onb1.t, in_=c.onorm.broadcast_to([128, D])), writes=onb1.k, dma=True)
    stg = [wa.carve(8192 + 2048 * q, 512, F32) for q in range(4)]
    blocks = []
    for b in range(2):
        blocks.append(("k", 1024 + b * 512, c.k1, b * 512))
    for b in range(4):
        blocks.append(("v", 2048 + b * 512, c.v1, b * 512))
    for b in range(4):
        blocks.append(("o", 4096 + b * 512, c.osig, b * 512))
    for b in range(4):
        blocks.append(("z", 6144 + b * 512, c.zsil, b * 512))
    cnt = 0
    for (kind, col0, dst, dc0) in blocks:
        wbh = WB[wbi % 2]
        wbi += 1
        load_w_block(c, c.w_in1, col0, 512, wbh)
        wv, wk = wb_views(wbh, 512)
        for i in range(NT):
            psb = c.ps[cnt % 2]
            sg = stg[cnt % 4]
            cnt += 1
            p.op("pe", seq(*[I("matmul", psb.t, lhsT=hmT[:, ch, i * 128:(i + 1) * 128], rhs=wv(ch),
                               start=(ch == 0), stop=(ch == 15)) for ch in range(16)]),
                 reads=[hmk[i]] + wk, writes=psb.k)
            if kind in ("k", "v"):
                o_ = sg.t.bitcast(BF16)[:, 0:512]
                if kind == "k":
                    p.op("act", I("activation", out=o_, in_=psb.t, func=AF.Copy, scale=128 ** -0.5),
                         reads=psb.k, writes=sg.k)
                else:
                    p.op("dve", I("tensor_copy", out=o_, in_=psb.t), reads=psb.k, writes=sg.k)
            else:
                o_ = sg.t
                p.op("act", I("activation", out=o_, in_=psb.t, func=(AF.Sigmoid if kind == "o" else AF.Silu)),
                     reads=psb.k, writes=sg.k)
                if kind == "z":
                    p.op("dve", I("tensor_tensor", out=o_, in0=o_, in1=onb1.t[:, dc0:dc0 + 512], op=ALU.mult),
                         reads=sg.k + onb1.k, writes=sg.k)
            p.op("sp", I("dma_start", out=dst[i * 128:(i + 1) * 128, dc0:dc0 + 512], in_=o_),
                 reads=sg.k, writes=[c.dtok(kind, i, dc0)], dma=True)
            gstep(1)
    for nb in range(4):
        for half in range(2):
            src = c.w_out1[half * 1024:(half + 1) * 1024, nb * 512:(nb + 1) * 512].rearrange("(c p) n -> p c n", p=128)
            p.op("pool", I("dma_start", out=hmT[:, half * 8:(half + 1) * 8, nb * 512:(nb + 1) * 512], in_=src),
                 writes=hmk + [c.dtok("wout1", nb, half)], dma=True)
    stop = getattr(c, "stop", None)
    if stop == "B":
        return
    if stop == "BF":
        emit_final(c)
        return
    for _ in gates_gen:
        pass
    if stop == "G":
        return
    emit_mlstm(c, mnf, mnb)
    if stop == "M":
        return
    emit_final(c)


def emit_gates(c):
    nc, p, wa = c.nc, c.p, c.wa
    K8 = 8192
    names = ["IP", "FP", "ONES", "T1", "T2", "F", "G", "PM", "NPM", "INTER", "WS"]
    B = {n: wa.carve(65536 + i * K8, 2048, F32, parts=40) for i, n in enumerate(names[:8])}
    B["NPM"], B["INTER"], B["WS"] = B["IP"], B["FP"], B["ONES"]
    gs = c.dtok("gsc")
    for n in ("IP", "FP"):
        p.op("pool", I("memset", B[n].t, 0.0), writes=B[n].k)
        yield
    p.op("pool", I("memset", B["ONES"].t, 1.0), writes=B["ONES"].k)
    yield
    p.op("sp", I("dma_start", out=B["IP"].t[0:8, :], in_=c.gsc[0:8, :]), reads=[gs], writes=B["IP"].k, dma=True)
    yield
    p.op("sp", I("dma_start", out=B["IP"].t[32:40, :], in_=c.gsc[8:16, :]), reads=[gs], writes=B["IP"].k, dma=True)
    yield
    p.op("sp", I("dma_start", out=B["FP"].t[0:8, :], in_=c.gsc[16:24, :]), reads=[gs], writes=B["FP"].k, dma=True)
    yield
    p.op("sp", I("dma_start", out=B["FP"].t[32:40, :], in_=c.gsc[24:32, :]), reads=[gs], writes=B["FP"].k, dma=True)
    yield
    IP, FP, ONES, T1, T2, F_, G_, PM, NPM, INTER, WS = [B[n] for n in names]

    def dve(fn, r, w):
        p.op("dve", fn, reads=[t for b in r for t in b.k], writes=[t for b in w for t in b.k])

    def act(fn, r, w):
        p.op("act", fn, reads=[t for b in r for t in b.k], writes=[t for b in w for t in b.k])
    act(I("activation", out=T1.t, in_=FP.t, func=AF.Abs), [FP], [T1])
    yield
    act(I("activation", out=T1.t, in_=T1.t, func=AF.Exp, scale=-1.0), [T1], [T1])
    yield
    act(I("activation", out=T1.t, in_=T1.t, func=AF.Ln, bias=1.0), [T1], [T1])
    yield
    dve(I("tensor_single_scalar", out=T2.t, in_=FP.t, scalar=0.0, op=ALU.min), [FP], [T2])
    yield
    dve(I("tensor_tensor", out=T2.t, in0=T2.t, in1=T1.t, op=ALU.subtract), [T2, T1], [T2])
    yield
    dve(I("tensor_tensor_scan", out=F_.t[0:8, :], data0=ONES.t[0:8, :], data1=T2.t[0:8, :], initial=0.0,
          op0=ALU.mult, op1=ALU.add), [ONES, T2], [F_])
    yield
    dve(I("tensor_tensor_scan", out=F_.t[32:40, ::-1], data0=ONES.t[32:40, ::-1], data1=T2.t[32:40, ::-1], initial=0.0,
          op0=ALU.mult, op1=ALU.add), [ONES, T2], [F_])
    yield
    dve(I("tensor_tensor", out=G_.t[0:8, :], in0=IP.t[0:8, :], in1=F_.t[0:8, :], op=ALU.subtract), [IP, F_], [G_])
    yield
    dve(I("tensor_tensor", out=G_.t[32:40, :], in0=IP.t[32:40, :], in1=F_.t[32:40, :], op=ALU.subtract), [IP, F_], [G_])
    yield
    dve(I("tensor_tensor_scan", out=PM.t[0:8, :], data0=G_.t[0:8, :], data1=G_.t[0:8, :], initial=0.0,
          op0=ALU.max, op1=ALU.max), [G_], [PM])
    yield
    dve(I("tensor_tensor_scan", out=PM.t[32:40, ::-1], data0=G_.t[32:40, ::-1], data1=G_.t[32:40, ::-1], initial=0.0,
          op0=ALU.max, op1=ALU.max), [G_], [PM])
    yield
    for r0 in (0, 32):
        dve(I("tensor_tensor", out=T1.t[r0:r0 + 8, :], in0=F_.t[r0:r0 + 8, :], in1=PM.t[r0:r0 + 8, :], op=ALU.add),
            [F_, PM], [T1])
        yield
        act(I("activation", out=T1.t[r0:r0 + 8, :], in_=T1.t[r0:r0 + 8, :], func=AF.Exp, scale=-1.0), [T1], [T1])
        yield
        p.op("pool", I("tensor_scalar", out=NPM.t[r0:r0 + 8, :], in0=PM.t[r0:r0 + 8, :], scalar1=-1.0, scalar2=None,
                       op0=ALU.mult), reads=PM.k, writes=NPM.k)
        yield
    EN = T1
    pme = c.sbt("g_pme", [40, 16], F32)
    pms = c.sbt("g_pms", [40, 16], F32)
    dec = c.sbt("g_dec", [40, 16], F32)
    PM3 = r3(PM.t, 128)
    G3 = r3(G_.t, 128)
    p.op("pool", I("memset", pms.t, 0.0), writes=pms.k)
    yield
    dve(I("tensor_copy", out=pme.t[0:8, :], in_=PM3[0:8, :, 127]), [PM], [pme])
    yield
    dve(I("tensor_copy", out=pme.t[32:40, :], in_=PM3[32:40, :, 0]), [PM], [pme])
    yield
    dve(I("tensor_copy", out=pms.t[0:8, 1:16], in_=pme.t[0:8, 0:15]), [pme], [pms])
    yield
    dve(I("tensor_copy", out=pms.t[32:40, 0:15], in_=pme.t[32:40, 1:16]), [pme], [pms])
    yield
    for r0 in (0, 32):
        sl = slice(r0, r0 + 8)
        dve(I("tensor_tensor", out=r3(T2.t, 128)[sl], in0=PM3[sl], in1=bc_mid(pms.t[sl, :], 128), op=ALU.subtract),
            [PM, pms], [T2])
        yield
        act(I("activation", out=INTER.t[sl, :], in_=T2.t[sl, :], func=AF.Exp, scale=-1.0), [T2], [INTER])
        yield
        dve(I("tensor_tensor", out=r3(T2.t, 128)[sl], in0=G3[sl], in1=bc_mid(pme.t[sl, :], 128), op=ALU.subtract),
            [G_, pme], [T2])
        yield
        act(I("activation", out=WS.t[sl, :], in_=T2.t[sl, :], func=AF.Exp), [T2], [WS])
        yield
        dve(I("tensor_tensor", out=dec.t[sl, :], in0=pms.t[sl, :], in1=pme.t[sl, :], op=ALU.subtract), [pms, pme], [dec])
        yield
        act(I("activation", out=dec.t[sl, :], in_=dec.t[sl, :], func=AF.Exp), [dec], [dec])
        yield
    for d, r0 in ((0, 0), (1, 32)):
        p.op("sp", I("dma_start", out=bass.AP(c.npm_d.tensor, d * 16 * 1024, [[128, 8], [1024, 16], [1, 128]]),
                     in_=r3(NPM.t, 128)[r0:r0 + 8]), reads=NPM.k, writes=[c.dtok("npm", d)], dma=True)
        yield
        p.op("sp", I("dma_start", out=bass.AP(c.inter_d.tensor, d * 16 * 1024, [[128, 8], [1024, 16], [1, 128]]),
                     in_=r3(INTER.t, 128)[r0:r0 + 8]), reads=INTER.k, writes=[c.dtok("inter", d)], dma=True)
        yield
        p.op("sp", I("dma_start", out=c.dec_d[0:1, d * 128:(d + 1) * 128].rearrange("o (h c) -> (o h) c", c=16),
                     in_=dec.t[r0:r0 + 8, :]), reads=dec.k, writes=[c.dtok("dec")], dma=True)
        yield
    c.Gcol = c.sbt("Gcol", [128, 256], F32)
    c.WScol = c.sbt("WScol", [128, 256], F32)
    c.ENcol = c.sbt("ENcol", [128, 256], F32)
    c.decbc = c.sbt("decbc", [128, 256], F32)
    p.op("sp", I("dma_start", out=c.decbc.t, in_=c.dec_d.broadcast_to([128, 256])),
         reads=[c.dtok("dec")], writes=c.decbc.k, dma=True)
    yield
    k = 0
    for src, dstc in ((G_, c.Gcol), (WS, c.WScol), (EN, c.ENcol)):
        for d, r0 in ((0, 0), (1, 32)):
            psb = c.ps[k % 2]
            k += 1
            p.op("pe", seq(*[I("transpose", out=psb.t[:, ch * 8:(ch + 1) * 8], in_=src.t[r0:r0 + 8, ch * 128:(ch + 1) * 128],
                               identity=c.cidf.t[r0:r0 + 8, r0:r0 + 8]) for ch in range(16)]),
                 reads=src.k + c.cidf.k, writes=psb.k)
            yield
            p.op("act", I("activation", out=dstc.t[:, d * 128:(d + 1) * 128], in_=psb.t[:, 0:128], func=AF.Copy),
                 reads=psb.k, writes=dstc.k)
            yield
    if c.debug:
        for nm, b in (("dGcol", c.Gcol), ("dWScol", c.WScol), ("dENcol", c.ENcol), ("ddecbc", c.decbc)):
            ap = nc.dram_tensor(nm, [128, 256], F32, kind="ExternalOutput").ap()
            p.op("sp", I("dma_start", out=ap, in_=b.t), reads=b.k, dma=True)
            yield


def emit_mlstm(c, mnf, mnb):
    nc, p, wa = c.nc, c.p, c.wa
    hmT, hmk = c.hmT, c.hmk
    NV = 258
    NVF = 320
    NVB = 384
    C32 = [wa.private(h * NVF * 4, NV, F32) for h in range(8)]
    o0 = 8 * NVF * 4
    Cb = [wa.private(o0 + h * NVB * 2, NV, BF16) for h in range(8)]
    o1 = o0 + 8 * NVB * 2
    SETB = 18688
    LD = []
    for s_ in range(3):
        b = o1 + s_ * SETB
        LD.append(dict(qT=wa.carve(b, 1024, BF16), kT=wa.carve(b + 2048, 1024, BF16), k=wa.carve(b + 4096, 1024, BF16),
                       V=wa.carve(b + 6144, 8 * NV, BF16), NPM=wa.carve(b + 6144 + 4352, 1024, F32),
                       INT=wa.carve(b + 6144 + 4352 + 4096, 1024, F32)))
    o2 = o1 + 3 * SETB
    NPMm = wa.carve(o2, 1024, F32)
    qpc = wa.carve(o2 + 4096, 1024, BF16)
    Vw = wa.carve(o2 + 6144, 8 * NV, BF16)
    DT = wa.carve(o2 + 10496, 1024, F32)
    scT = wa.carve(o2 + 14592, 1024, BF16)
    o3 = o2 + 16640
    hb = [wa.private(o3 + h * NVF * 4, NV, F32) for h in range(8)]
    hb_all = wa._ap(o3, 8 * NVF, F32, 128)[0]
    hout = wa.carve(o3 + 10240, 2048, F32)
    o4 = o3 + 10240 + 8192
    Vst = wa.carve(o4, 2048, BF16)
    assert o4 + 4096 <= wa.nbytes
    for s_ in range(3):
        p.op("pool", I("memset", LD[s_]["V"].t, 1.0), writes=LD[s_]["V"].k)
    st_cols, stt = statcols(c, "m_st", 96)
    mring = [Buf(stt[:, j * 8:(j + 1) * 8], [Tok("ms")]) for j in range(12)]
    stn = [0]

    def st8():
        j = stn[0]
        stn[0] += 1
        return mring[j % 12]
    qk_toks = [c.dtok("qk", j) for j in range(16)]
    hc = [0]
    steps = []
    for d in getattr(c, "ml_dirs", (0, 1)):
        order = list(range(16)) if d == 0 else list(range(15, -1, -1))
        for ch in order[:getattr(c, "ml_steps", 16)]:
            steps.append((d, ch))

    def loads(gs):
        d, ch = steps[gs]
        L = LD[gs % 3]
        cs = slice(ch * 128, (ch + 1) * 128)
        qT3, kT3, V3 = r3(L["qT"].t, 128), r3(L["kT"].t, 128), r3(L["V"].t, NV)
        p.op("sp", I("dma_start", out=L["qT"].t, in_=c.qT1[ch]), reads=qk_toks[0:8], writes=L["qT"].k, dma=True)
        p.op("sp", I("dma_start", out=L["kT"].t, in_=c.kT1[ch]), reads=qk_toks[8:16], writes=L["kT"].k, dma=True)
        p.op("sp", I("dma_start", out=L["k"].t, in_=c.k1[cs, :]),
             reads=[c.dtok("k", ch, 0), c.dtok("k", ch, 512)], writes=L["k"].k, dma=True)
        p.op("sp", I("dma_start", out=Vst.t, in_=c.v1[cs, :]),
             reads=[c.dtok("v", ch, q * 512) for q in range(4)], writes=Vst.k, dma=True)
        p.op("sp", I("dma_start", out=L["NPM"].t,
                     in_=bass.AP(c.npm_d.tensor, (d * 16 + ch) * 1024, [[0, 128], [1, 1024]])),
             reads=[c.dtok("npm", d)], writes=L["NPM"].k, dma=True)
        p.op("sp", I("dma_start", out=L["INT"].t,
                     in_=bass.AP(c.inter_d.tensor, (d * 16 + ch) * 1024, [[0, 128], [1, 1024]])),
             reads=[c.dtok("inter", d)], writes=L["INT"].k, dma=True)

    def vcopy(gs):
        L = LD[gs % 3]
        V3 = r3(L["V"].t, NV)
        p.op("act", I("activation", out=V3[:, :, 0:256], in_=r3(Vst.t, 256), func=AF.Copy), reads=Vst.k, writes=L["V"].k)

    def ctx(gs):
        d, ch = steps[gs]
        L = LD[gs % 3]
        return d, ch, L, r3(L["qT"].t, 128), r3(L["kT"].t, 128), r3(L["V"].t, NV), d * 128 + ch * 8

    def frontA(gs):
        d, ch, L, qT3, kT3, V3, cb0 = ctx(gs)
        mask = mnf if d == 0 else mnb
        p.op("dve", I("tensor_tensor", out=r3(NPMm.t, 128), in0=r3(L["NPM"].t, 128),
                      in1=mask.t.unsqueeze(1).broadcast_to([128, 8, 128]), op=ALU.add),
             reads=L["NPM"].k + mask.k, writes=NPMm.k)
        p.op("pe", seq(*[I("matmul", c.ps[h // 4].t[:, (h % 4) * 128:(h % 4 + 1) * 128], lhsT=kT3[:, h, :],
                           rhs=qT3[:, h, :], start=True, stop=True) for h in range(8)]),
             reads=L["qT"].k + L["kT"].k, writes=c.ps[0].k + c.ps[1].k)
        p.op("dve", I("tensor_tensor", out=r3(NPMm.t, 128), in0=r3(NPMm.t, 128),
                      in1=bc_mid(c.Gcol.t[:, cb0:cb0 + 8], 128), op=ALU.add),
             reads=NPMm.k + c.Gcol.k, writes=NPMm.k)

    def frontA2(gs):
        p.op("act", I("activation", out=DT.t, in_=NPMm.t, func=AF.Exp), reads=NPMm.k, writes=DT.k)

    def frontB(gs):
        d, ch, L, qT3, kT3, V3, cb0 = ctx(gs)
        p.op("dve", I("tensor_tensor", out=qpc.t, in0=L["qT"].t, in1=L["INT"].t, op=ALU.mult),
             reads=L["qT"].k + L["INT"].k, writes=qpc.k)
        p.op("pool", I("tensor_tensor", out=r3(Vw.t[:, 0:1024], 128), in0=r3(L["k"].t, 128),
                       in1=bc_mid(c.WScol.t[:, cb0:cb0 + 8], 128), op=ALU.mult),
             reads=L["k"].k + c.WScol.k, writes=Vw.k)
        for half in range(2):
            p.op("dve", I("tensor_tensor", out=scT.t[:, half * 512:(half + 1) * 512], in0=c.ps[half].t,
                          in1=DT.t[:, half * 512:(half + 1) * 512], op=ALU.mult),
                 reads=c.ps[half].k + DT.k, writes=scT.k)

    def compute(gs):
        d, ch, L, qT3, kT3, V3, cb0 = ctx(gs)
        if gs == 0 or steps[gs - 1][0] != d:
            for h in range(8):
                p.op("pool", I("memset", C32[h].t, 0.0), writes=C32[h].k)
                p.op("pool", I("memset", Cb[h].t, 0.0), writes=Cb[h].k)
        cs = slice(ch * 128, (ch + 1) * 128)
        Vw3 = r3(Vw.t, NV)
        if True:
            scT3, qpc3 = r3(scT.t, 128), r3(qpc.t, 128)
            for h in range(8):
                psN = c.ps[2 + (hc[0] % 3)]
                psU = c.ps[5 + (hc[0] % 2)]
                hc[0] += 1
                p.op("pe", seq(I("matmul", psN.t[:, 0:257], lhsT=scT3[:, h, :], rhs=V3[:, h, 0:257], start=True, stop=False),
                               I("matmul", psN.t[:, 0:257], lhsT=qpc3[:, h, :], rhs=Cb[h].t[:, 0:257], start=False, stop=True)),
                     reads=scT.k + L["V"].k + qpc.k + Cb[h].k, writes=psN.k)
                if h % 2 == 0:
                    p.op("act", I("activation", out=hb[h].t[:, 0:257], in_=psN.t[:, 0:257], func=AF.Copy),
                         reads=psN.k, writes=hb[h].k)
                else:
                    p.op("dve", I("tensor_copy", out=hb[h].t[:, 0:257], in_=psN.t[:, 0:257]),
                         reads=psN.k, writes=hb[h].k)
                p.op("pe", I("matmul", psU.t[:, 0:257], lhsT=Vw.t[:, h * 128:(h + 1) * 128], rhs=V3[:, h, 0:257],
                             start=True, stop=True), reads=L["V"].k + Vw.k, writes=psU.k)
                dcol = d * 128 + h * 16 + ch
                p.op("dve", I("scalar_tensor_tensor", out=C32[h].t[:, 0:257], in0=C32[h].t[:, 0:257],
                              scalar=c.decbc.t[:, dcol:dcol + 1], in1=psU.t[:, 0:257], op0=ALU.mult, op1=ALU.add),
                     reads=C32[h].k + c.decbc.k + psU.k, writes=C32[h].k)
                p.op("pool", I("tensor_copy", out=Cb[h].t[:, 0:257], in_=C32[h].t[:, 0:257]),
                     reads=C32[h].k, writes=Cb[h].k)
            if gs + 1 < len(steps):
                frontA2(gs + 1)
                frontB(gs + 1)
            hb3 = r3(hb_all, NVF)
            hbk = [t for b in hb for t in b.k]
            dn, rc = st8(), st8()
            p.op("act", I("activation", out=dn.t, in_=hb3[:, :, 256], func=AF.Abs), reads=hbk, writes=dn.k)
            p.op("dve", I("tensor_tensor", out=dn.t, in0=dn.t, in1=c.ENcol.t[:, cb0:cb0 + 8], op=ALU.max),
                 reads=dn.k + c.ENcol.k, writes=dn.k)
            p.op("dve", I("reciprocal", out=rc.t, in_=dn.t), reads=dn.k, writes=rc.k)
            ho3 = r3(hout.t, 256)
            p.op("act", seq(*[I("activation", out=ho3[:, h, :], in_=hb3[:, h, 0:256], func=AF.Copy, scale=rc.t[:, h:h + 1])
                              for h in range(0, 8, 2)]), reads=hbk + rc.k, writes=hout.k)
            p.op("dve", I("tensor_tensor", out=ho3[:, 1::2, :], in0=hb3[:, 1::2, 0:256], in1=bc_mid(rc.t[:, 1::2], 256),
                          op=ALU.mult), reads=hbk + rc.k, writes=hout.k)
            dst = c.hF if d == 0 else c.hB
            p.op("sp", I("dma_start", out=dst[cs, :], in_=hout.t), reads=hout.k,
                 writes=[c.dtok("hF" if d == 0 else "hB", ch)], dma=True)

    pend = [None]
    loads(0)
    vcopy(0)
    if len(steps) > 1:
        loads(1)
    frontA(0)
    frontA2(0)
    frontB(0)
    for gs in range(len(steps)):
        if gs + 1 < len(steps):
            frontA(gs + 1)
        compute(gs)
        if gs + 1 < len(steps):
            vcopy(gs + 1)
        if gs + 2 < len(steps):
            loads(gs + 2)
    wa.release()


def emit_final(c):
    nc, p, wa = c.nc, c.p, c.wa
    hmT, hmk = c.hmT, c.hmk
    wtok = [c.dtok("wout1", nb, half) for nb in range(4) for half in range(2)]
    K8 = 8192
    hFb = [wa.carve(0, 2048, F32), wa.carve(K8, 2048, F32)]
    hBb = [wa.carve(2 * K8, 2048, F32), wa.carve(3 * K8, 2048, F32)]
    osb = [wa.carve(4 * K8, 2048, F32), wa.carve(5 * K8, 2048, F32)]
    zsb = [wa.carve(6 * K8, 2048, F32), wa.carve(7 * K8, 2048, F32)]
    x1tb = [wa.carve(8 * K8, 2048, F32), wa.carve(15 * K8, 2048, F32)]
    x2 = wa.carve(9 * K8, 2048, F32)
    yb = wa.carve(10 * K8, 2048, F32)
    fbc = wa.carve(11 * K8, 2048, F32)
    junk = wa.carve(12 * K8, 2048, BF16)
    hsb = wa.carve(12 * K8 + 4096, 2048, BF16)
    mt = [wa.carve(13 * K8, 2048, BF16), wa.carve(13 * K8 + 4096, 2048, BF16), wa.carve(14 * K8, 2048, BF16)]
    p.op("sp", I("dma_start", out=fbc.t, in_=c.fgain.broadcast_to([128, D])), writes=fbc.k, dma=True)
    sc, sct = statcols(c, "F_st", 64 + 16 * 32)
    fused = ("x1", 0, 0) in c.dtoks

    def loads(i):
        cs = slice(i * 128, (i + 1) * 128)
        p.op("sp", I("dma_start", out=hFb[i % 2].t, in_=c.hF[cs, :]), reads=[c.dtok("hF", i)], writes=hFb[i % 2].k, dma=True)
        p.op("sp", I("dma_start", out=hBb[i % 2].t, in_=c.hB[cs, :]), reads=[c.dtok("hB", i)], writes=hBb[i % 2].k, dma=True)
        p.op("sp", I("dma_start", out=osb[i % 2].t, in_=c.osig[cs, :]), reads=[c.dtok("o", i, q * 512) for q in range(4)],
             writes=osb[i % 2].k, dma=True)
        p.op("sp", I("dma_start", out=zsb[i % 2].t, in_=c.zsil[cs, :]), reads=[c.dtok("z", i, q * 512) for q in range(4)],
             writes=zsb[i % 2].k, dma=True)

    fst = [[Buf(sct[:, 64 + i * 32 + 8 * q:64 + i * 32 + 8 * q + 8], [Tok("fs")]) for q in range(4)] for i in range(NT)]

    def finalize(i):
        hF_, hB_, os_, zs_ = hFb[i % 2], hBb[i % 2], osb[i % 2], zsb[i % 2]
        p.op("dve", I("tensor_tensor", out=hB_.t, in0=hB_.t, in1=hF_.t, op=ALU.add), reads=hB_.k + hF_.k, writes=hB_.k)
        p.op("pool", I("tensor_tensor", out=hB_.t, in0=hB_.t, in1=os_.t, op=ALU.mult), reads=hB_.k + os_.k, writes=hB_.k)
        p.op("act", I("activation", out=hF_.t, in_=hB_.t, func=AF.Square), reads=hB_.k, writes=hF_.k)

    def fin2(i):
        hF_, hB_, os_, zs_ = hFb[i % 2], hBb[i % 2], osb[i % 2], zsb[i % 2]
        ss, rs, rs2, rstd = fst[i]
        p.op("dve", I("tensor_reduce", out=ss.t, in_=r3(hF_.t, 256), axis=AX.X, op=ALU.add), reads=hF_.k, writes=ss.k)
        p.op("dve", I("tensor_scalar", out=rs.t, in0=ss.t, scalar1=1.0 / 256, scalar2=EPS, op0=ALU.mult, op1=ALU.add),
             reads=ss.k, writes=rs.k)
        p.op("act", I("activation", out=rs2.t, in_=rs.t, func=AF.Sqrt), reads=rs.k, writes=rs2.k)
        p.op("dve", I("reciprocal", out=rstd.t, in_=rs2.t), reads=rs2.k, writes=rstd.k)
        ho3 = r3(hB_.t, 256)
        p.op("act", seq(*[I("activation", out=ho3[:, h, :], in_=ho3[:, h, :], func=AF.Copy, scale=rstd.t[:, h:h + 1])
                          for h in range(8)]), reads=hB_.k + rstd.k, writes=hB_.k)
        p.op("dve", I("tensor_tensor", out=hsb.t, in0=hB_.t, in1=zs_.t, op=ALU.mult), reads=hB_.k + zs_.k, writes=hsb.k)

    def fin_tr(i):
        m_ = mt[i % 3]
        for g in range(4):
            pb = c.ps[6 + (g % 2)]
            pT = pb.t.bitcast(BF16)
            p.op("pe", seq(*[I("transpose", out=pT[:, q * 128:(q + 1) * 128],
                               in_=hsb.t[:, (4 * g + q) * 128:(4 * g + q + 1) * 128], identity=c.cidb.t)
                             for q in range(4)]), reads=hsb.k + c.cidb.k, writes=pb.k)
            p.op("act", I("activation", out=r3(m_.t, 128)[:, 4 * g:4 * g + 4, :], in_=r3(pT[:, 0:512], 128), func=AF.Copy),
                 reads=pb.k, writes=m_.k)

    def x1load(i):
        x1t = x1tb[i % 2]
        p.op("sp", I("dma_start", out=x1t.t, in_=c.x1[i * 128:(i + 1) * 128, :]),
             reads=([c.dtok("x1", i, nb) for nb in range(4)] if fused else []), writes=x1t.k, dma=True)

    def project(i):
        m3 = r3(mt[i % 3].t, 128)
        for nb in range(4):
            psb = c.ps[(4 * i + nb) % 6]
            p.op("pe", seq(*[I("matmul", psb.t, lhsT=m3[:, ch, :], rhs=hmT[:, ch, nb * 512:(nb + 1) * 512],
                               start=(ch == 0), stop=(ch == 15)) for ch in range(16)]),
                 reads=mt[i % 3].k + wtok, writes=psb.k)

    def proj_post(i):
        x1t = x1tb[i % 2]
        ss, rs, rs2, rstd = sc[4 * i:4 * i + 4]
        for nb in range(4):
            psb = c.ps[(4 * i + nb) % 6]
            p.op("dve", I("tensor_tensor", out=x2.t[:, nb * 512:(nb + 1) * 512], in0=psb.t,
                          in1=x1t.t[:, nb * 512:(nb + 1) * 512], op=ALU.add), reads=psb.k + x1t.k, writes=x2.k)
        p.op("act", I("activation", out=junk.t, in_=x2.t, func=AF.Square, accum_out=ss.t), reads=x2.k, writes=junk.k + ss.k)
        p.op("dve", I("tensor_scalar", out=rs.t, in0=ss.t, scalar1=1.0 / D, scalar2=EPS, op0=ALU.mult, op1=ALU.add),
             reads=ss.k, writes=rs.k)
        p.op("act", I("activation", out=rs2.t, in_=rs.t, func=AF.Sqrt), reads=rs.k, writes=rs2.k)
        p.op("dve", I("reciprocal", out=rstd.t, in_=rs2.t), reads=rs2.k, writes=rstd.k)
        p.op("act", I("activation", out=yb.t, in_=x2.t, func=AF.Copy, scale=rstd.t), reads=x2.k + rstd.k, writes=yb.k)
        p.op("pool", I("tensor_tensor", out=yb.t, in0=yb.t, in1=fbc.t, op=ALU.mult), reads=yb.k + fbc.k, writes=yb.k)
        p.op("pool", I("dma_start", out=c.out[i * 128:(i + 1) * 128, :], in_=yb.t), reads=yb.k, writes=[c.dtok("out", i)], dma=True)

    loads(0)
    loads(1)
    finalize(0)
    fin2(0)
    fin_tr(0)
    loads(2)
    finalize(1)
    fin2(1)
    fin_tr(1)
    loads(3)
    x1load(0)
    for i in range(NT):
        if i + 1 < NT:
            x1load(i + 1)
        project(i)
        if 2 <= i + 1 < NT:
            fin_tr(i + 1)
        if i + 2 < NT:
            finalize(i + 2)
        proj_post(i)
        if i + 2 < NT:
            fin2(i + 2)
        if i + 4 < NT:
            loads(i + 4)
```

```python
import numpy as np
import ml_dtypes
import concourse.bass as bass
import concourse.mybir as mybir
from concourse.bass_utils import run_bass_kernel_spmd

F32 = mybir.dt.float32
BF16 = mybir.dt.bfloat16
AF = mybir.ActivationFunctionType
ALU = mybir.AluOpType
AX = mybir.AxisListType


class Tok:
    __slots__ = ("name", "w", "r")

    def __init__(self, name=""):
        self.name = name
        self.w = None
        self.r = []


class Op:
    __slots__ = ("eng", "fn", "pos", "is_dma", "waits", "dma_waits", "signal", "sigval",
                 "dsem", "dval", "dprev", "known_after", "gid")


ENGS = ("pe", "act", "dve", "pool", "sp")


class Prog:
    def __init__(self, nc, n_dma_sems=16):
        self.nc = nc
        self.ops = {e: [] for e in ENGS}
        self.known = {e: {x: -1 for x in ENGS} for e in ENGS}
        self.dma_seen = {e: set() for e in ENGS}
        self.n_dma_sems = n_dma_sems
        self.dma_count = {e: 0 for e in ENGS}
        self.gid = 0

    def op(self, eng, fn, reads=(), writes=(), dma=False):
        o = Op()
        o.eng = eng
        o.fn = fn
        o.is_dma = dma
        o.pos = len(self.ops[eng])
        o.signal = False
        o.sigval = None
        o.gid = self.gid
        self.gid += 1
        deps = []
        for t in reads:
            if t.w is not None:
                deps.append(t.w)
        for t in writes:
            if t.w is not None:
                deps.append(t.w)
            deps.extend(t.r)
        known = self.known[eng]
        seen = self.dma_seen[eng]
        waits = []
        dma_waits = []
        for d in sorted(set(deps), key=lambda z: -z.gid):
            if d is o:
                continue
            if d.is_dma:
                if d.gid in seen:
                    continue
                seen.add(d.gid)
                dma_waits.append(d)
                continue
            if d.eng == "pe" and eng == "pe":
                continue
            if known[d.eng] >= d.pos:
                continue
            waits.append(d)
            d.signal = True
            for e2, v in d.known_after.items():
                if v > known[e2]:
                    known[e2] = v
        o.waits = waits
        o.dma_waits = dma_waits
        if dma:
            n = self.dma_count[eng]
            self.dma_count[eng] += 1
            o.dsem = n % self.n_dma_sems
            o.dval = 16 * (n // self.n_dma_sems + 1)
            o.dprev = 16 * (n // self.n_dma_sems)
            o.known_after = None
        else:
            ka = dict(known)
            ka[eng] = o.pos
            o.known_after = ka
        for t in reads:
            t.r.append(o)
        for t in writes:
            t.w = o
            t.r = []
        self.ops[eng].append(o)
        return o

    def emit(self):
        nc = self.nc
        import contextlib
        with contextlib.ExitStack() as st:
            csem = {e: st.enter_context(nc.semaphore("cs_" + e)) for e in ENGS if e != "sp"}
            dsem = {e: [st.enter_context(nc.semaphore("ds_%s_%d" % (e, i))) for i in range(self.n_dma_sems)]
                    for e in ENGS if self.dma_count[e] > 0}
            for e in ENGS:
                c = 0
                for o in self.ops[e]:
                    if o.is_dma:
                        continue
                    if o.signal:
                        c += 1
                        o.sigval = c
            final_waits = []
            for e in ENGS:
                for o in self.ops[e]:
                    if o.is_dma:
                        final_waits.append((dsem[e][o.dsem], o.dval))
            block = st.enter_context(nc.Block())

            def run(eng_key, eng):
                for o in self.ops[eng_key]:
                    for d in o.waits:
                        eng.wait_ge(csem[d.eng], d.sigval)
                    for d in o.dma_waits:
                        eng.wait_ge(dsem[d.eng][d.dsem], d.dval)
                    if o.is_dma:
                        if o.dprev > 0:
                            eng.wait_ge(dsem[eng_key][o.dsem], o.dprev)
                        ins = o.fn(eng)
                        ins.then_inc(dsem[eng_key][o.dsem], 16)
                    else:
                        ins = o.fn(eng)
                        if o.signal:
                            ins.then_inc(csem[eng_key], 1)
                if eng_key == "sp":
                    last = {}
                    for s, v in final_waits:
                        k = id(s)
                        if k not in last or last[k][1] < v:
                            last[k] = (s, v)
                    for s, v in last.values():
                        eng.wait_ge(s, v)

            @block.tensor
            def _(eng):
                run("pe", eng)

            @block.scalar
            def _(eng):
                run("act", eng)

            @block.vector
            def _(eng):
                run("dve", eng)

            @block.gpsimd
            def _(eng):
                run("pool", eng)

            @block.sync
            def _(eng):
                run("sp", eng)


def I(method, *args, **kw):
    return lambda e: getattr(e, method)(*args, **kw)


def seq(*fns):
    def f(e):
        r = None
        for g in fns:
            r = g(e)
        return r
    return f


class Buf:
    __slots__ = ("t", "k")

    def __init__(self, t, k):
        self.t = t
        self.k = k


class Arena:
    def __init__(self, nc, st, name, nbytes, gran=2048):
        self.n = nbytes // 2
        self.t = st.enter_context(nc.sbuf_tensor(name, [128, self.n], BF16))
        self.gran = gran
        self.toks = [Tok("%s_%d" % (name, i)) for i in range((nbytes + gran - 1) // gran)]
        self.nbytes = nbytes
        self._priv = []

    def _ap(self, off, nelem, dt, parts):
        sz = 4 if dt == F32 else 2
        nb = nelem * sz
        assert off % 4 == 0 and off + nb <= self.nbytes, (off, nb, self.nbytes)
        ap = self.t[0:parts, off // 2:(off + nb) // 2]
        if dt == F32:
            ap = ap.bitcast(F32)
        return ap, nb

    def private(self, off, nelem, dt, parts=128):
        ap, nb = self._ap(off, nelem, dt, parts)
        t = Tok("priv")
        for g in self.toks[off // self.gran:(off + nb - 1) // self.gran + 1]:
            if g.w is not None:
                t.r.append(g.w)
            t.r.extend(g.r)
        b = Buf(ap, [t])
        self._priv.append((b, off, nb))
        return b

    def release(self):
        for b, off, nb in self._priv:
            for g in self.toks[off // self.gran:(off + nb - 1) // self.gran + 1]:
                for t in b.k:
                    if t.w is not None:
                        g.r.append(t.w)
                    g.r.extend(t.r)
        self._priv = []

    def carve(self, off, nelem, dt, parts=128):
        sz = 4 if dt == F32 else 2
        nb = nelem * sz
        assert off % 4 == 0 and off + nb <= self.nbytes, (off, nb, self.nbytes)
        ap = self.t[0:parts, off // 2:(off + nb) // 2]
        if dt == F32:
            ap = ap.bitcast(F32)
        toks = self.toks[off // self.gran:(off + nb - 1) // self.gran + 1]
        return Buf(ap, list(toks))


EPS = 1e-6
S = 2048
D = 2048
NT = 16
EVEN_IN = 4608
ODD_IN = 8224


class Ctx:
    def __init__(self):
        self.dtoks = {}

    def dtok(self, *key):
        if key not in self.dtoks:
            self.dtoks[key] = Tok(str(key))
        return self.dtoks[key]


def build_program(mode="fused", debug=False, stop=None):
    import contextlib
    nc = bass.Bass("TRN2", target_bir_lowering=False)
    c = Ctx()
    c.nc = nc
    c.debug = debug
    c.dbg = {}
    c.stop = stop
    import os
    if os.environ.get("ML_DIRS"):
        c.ml_dirs = tuple(int(v) for v in os.environ["ML_DIRS"].split(","))
    if os.environ.get("ML_STEPS"):
        c.ml_steps = int(os.environ["ML_STEPS"])

    def din(name, shape, dt=F32):
        return nc.dram_tensor(name, list(shape), dt, kind="ExternalInput").ap()

    def dout(name, shape, dt=F32):
        return nc.dram_tensor(name, list(shape), dt, kind="ExternalOutput").ap()

    def dscr(name, shape, dt=F32):
        if debug:
            ap = nc.dram_tensor(name, list(shape), dt, kind="ExternalOutput").ap()
            c.dbg[name] = ap
            return ap
        return nc.dram_tensor(name, list(shape), dt).ap()

    do0 = mode in ("l0", "fused")
    do1 = mode in ("l1", "fused")
    c.x = din("x", [S, D])
    c.identb = din("identb", [128, 128], BF16)
    c.identf = din("identf", [128, 128])
    if do0:
        c.cosT = din("cosT", [S, 64])
        c.sinT = din("sinT", [S, 64])
        c.gT0 = din("gT0", [128, 16])
        c.w_in0 = din("w_in0", [D, EVEN_IN])
        c.w_out0 = din("w_out0", [D, D])
        c.q_gain = din("q_gain", [1, 128])
        c.k_gain = din("k_gain", [1, 128])
        c.cw = din("cw", [128, 8, 4])
        c.cb = din("cb", [128, 8])
        c.lru_wa = din("lru_wa", [2, 8, 128, 128])
        c.lru_wx = din("lru_wx", [2, 8, 128, 128])
        c.lba = din("lba", [128, 16])
        c.lbx = din("lbx", [128, 16])
        c.llam = din("llam", [128, 16])
        c.gA = dscr("gA", [1024, S])
        c.xl = dscr("xl", [1024, S])
        c.gL = dscr("gL", [1024, S])
    if do1:
        c.gT1 = din("gT1", [128, 16])
        c.w_in1 = din("w_in1", [D, ODD_IN])
        c.w_out1 = din("w_out1", [D, D])
        c.gate_bias = din("gate_bias", [32, 1])
        c.onorm = din("onorm", [1, D])
        c.fgain = din("fgain", [1, D])
        c.mnegf = din("mnegf", [128, 128])
        c.mnegb = din("mnegb", [128, 128])
        for nm, shp, dt_ in (("qT1", [16, 128, 1024], BF16), ("kT1", [16, 128, 1024], BF16), ("k1", [S, 1024], BF16),
                             ("v1", [S, D], BF16), ("osig", [S, D], F32), ("zsil", [S, D], F32),
                             ("gsc", [32, S], F32), ("npm_d", [32, 1024], F32), ("inter_d", [32, 1024], F32),
                             ("dec_d", [1, 256], F32), ("hF", [S, D], F32), ("hB", [S, D], F32)):
            setattr(c, nm, dscr(nm, shp, dt_))
        c.out = dout("out", [S, D])
    if mode == "l0":
        c.x1 = dout("x1", [S, D])
    elif mode == "fused":
        c.x1 = dscr("x1", [S, D])
    else:
        c.x1 = c.x

    p = Prog(nc)
    c.p = p
    with contextlib.ExitStack() as st:
        c.st = st

        def sbt(name, shape, dt):
            t = st.enter_context(nc.sbuf_tensor(name, list(shape), dt))
            return Buf(t[:], [Tok(name)])
        c.sbt = sbt
        c.hm = Arena(nc, st, "hm", 65536, gran=4096)
        c.wa = Arena(nc, st, "wa", 131072, gran=256)
        c.ps = []
        for i in range(8):
            t = st.enter_context(nc.psum_tensor("psb%d" % i, [128, 512], F32))
            c.ps.append(Buf(t[:], [Tok("ps%d" % i)]))
        c.cidb = sbt("c_identb", [128, 128], BF16)
        c.cidf = sbt("c_identf", [128, 128], F32)
        c.onesb = sbt("c_onesb", [128, 128], BF16)
        c.one_col = sbt("c_one", [128, 1], F32)
        c.mone_col = sbt("c_mone", [128, 1], F32)
        c.eps_col = sbt("c_eps", [128, 1], F32)
        p.op("pool", I("memset", c.eps_col.t, EPS), writes=c.eps_col.k)
        p.op("pool", I("memset", c.one_col.t, 1.0), writes=c.one_col.k)
        p.op("pool", I("memset", c.mone_col.t, -1.0), writes=c.mone_col.k)
        p.op("sp", I("dma_start", out=c.cidb.t, in_=c.identb), writes=c.cidb.k, dma=True)
        p.op("sp", I("dma_start", out=c.cidf.t, in_=c.identf), writes=c.cidf.k, dma=True)
        p.op("pool", I("memset", c.onesb.t, 1.0), writes=c.onesb.k)
        if do0:
            emit_layer0(c)
        if do1:
            emit_layer1(c)
        p.emit()
    return nc, c


def r3(ap, inner):
    return ap.rearrange("p (a b) -> p a b", b=inner)


def bc_mid(ap2, n):
    return ap2.unsqueeze(2).broadcast_to([ap2.shape[0], ap2.shape[1], n])


def statcols(c, name, n, parts=128):
    t = c.st.enter_context(c.nc.sbuf_tensor(name, [128, n], F32))
    return [Buf(t[0:parts, j:j + 1], [Tok("%s%d" % (name, j))]) for j in range(n)], t


def phase_A(c, x_src, gT, tag, src_toks=None):
    p, wa = c.p, c.wa
    hmT = c.hmT
    xt = [wa.carve(0, 2048, F32), wa.carve(8192, 2048, F32), wa.carve(65536, 2048, F32)]
    junk = wa.carve(16384, 2048, BF16)
    xs = [wa.carve(20480, 2048, BF16), wa.carve(24576, 2048, BF16)]
    sc, _ = statcols(c, "A%s_st" % tag, 64)
    def ld(i):
        x_ = xt[i % 3]
        p.op("sp", I("dma_start", out=x_.t, in_=x_src[i * 128:(i + 1) * 128, :]),
             reads=(src_toks(i) if src_toks else []), writes=x_.k, dma=True)

    def stage1(i):
        x_ = xt[i % 3]
        ss, rs, rs2, rstd = sc[4 * i:4 * i + 4]
        p.op("act", I("activation", out=junk.t, in_=x_.t, func=AF.Square, accum_out=ss.t),
             reads=x_.k, writes=junk.k + ss.k)
        p.op("act", I("activation", out=rs2.t, in_=ss.t, func=AF.Ln, scale=1.0 / D, bias=c.eps_col.t),
             reads=ss.k + c.eps_col.k, writes=rs2.k)
        p.op("act", I("activation", out=rstd.t, in_=rs2.t, func=AF.Exp, scale=-0.5), reads=rs2.k, writes=rstd.k)

    def stage2(i):
        x_ = xt[i % 3]
        xs_ = xs[i % 2]
        ss, rs, rs2, rstd = sc[4 * i:4 * i + 4]
        p.op("act", I("activation", out=xs_.t, in_=x_.t, func=AF.Copy, scale=rstd.t),
             reads=x_.k + rstd.k, writes=xs_.k)
        for g in range(4):
            pb = c.ps[6 + (g % 2)]
            pT = pb.t.bitcast(BF16)
            p.op("pe", seq(*[I("transpose", out=pT[:, q * 128:(q + 1) * 128],
                               in_=xs_.t[:, (4 * g + q) * 128:(4 * g + q + 1) * 128], identity=c.cidb.t)
                             for q in range(4)]),
                 reads=xs_.k + c.cidb.k, writes=pb.k)
            p.op("dve", I("tensor_tensor", out=hmT[:, 4 * g:4 * g + 4, i * 128:(i + 1) * 128],
                          in0=r3(pT[:, 0:512], 128), in1=bc_mid(gT.t[:, 4 * g:4 * g + 4], 128), op=ALU.mult),
                 reads=pb.k + gT.k, writes=[c.hmk[i]])

    for i in range(3):
        ld(i)
    stage1(0)
    for i in range(NT):
        if i + 1 < NT:
            stage1(i + 1)
        stage2(i)
        if i + 3 < NT:
            ld(i + 3)


def load_w_block(c, wsrc, col0, ncols, wbh):
    p = c.p
    for half in range(2):
        b = wbh[half]
        src = wsrc[half * 1024:(half + 1) * 1024, col0:col0 + ncols].rearrange("(c p) n -> p c n", p=128)
        p.op("pool", I("dma_start", out=r3(b.t, ncols), in_=src), writes=b.k, dma=True)


def wb_views(wbh, ncols):
    v = [r3(wbh[0].t, ncols), r3(wbh[1].t, ncols)]
    return (lambda ch: v[ch // 8][:, ch % 8, :]), wbh[0].k + wbh[1].k


def emit_layer0(c):
    nc, p, wa = c.nc, c.p, c.wa
    c.hmT = r3(c.hm.t[:, :], S)
    c.hmk = [Tok("hm%d" % i) for i in range(NT)]
    hmT, hmk = c.hmT, c.hmk
    sbt = c.sbt
    cos = wa.carve(114688, 16 * 64, F32)
    sin = wa.carve(118784, 16 * 64, F32)
    cos.t = r3(cos.t, 64)
    sin.t = r3(sin.t, 64)
    gT0 = sbt("gT0s", [128, 16], F32)
    qgb = sbt("qgb", [128, 512], F32)
    kgb = sbt("kgb", [128, 256], F32)
    p.op("sp", I("dma_start", out=cos.t, in_=c.cosT.rearrange("(i p) f -> p i f", p=128)), writes=cos.k, dma=True)
    p.op("sp", I("dma_start", out=sin.t, in_=c.sinT.rearrange("(i p) f -> p i f", p=128)), writes=sin.k, dma=True)
    p.op("sp", I("dma_start", out=gT0.t, in_=c.gT0), writes=gT0.k, dma=True)
    p.op("sp", I("dma_start", out=r3(qgb.t, 128), in_=bass.AP(c.q_gain.tensor, 0, [[0, 128], [0, 4], [1, 128]])),
         writes=qgb.k, dma=True)
    p.op("sp", I("dma_start", out=r3(kgb.t, 128), in_=bass.AP(c.k_gain.tensor, 0, [[0, 128], [0, 2], [1, 128]])),
         writes=kgb.k, dma=True)

    phase_A(c, c.x, gT0, "0")

    WB = [[wa.carve(32768, 8 * 512, BF16), wa.carve(32768 + 8192, 8 * 512, BF16)],
          [wa.carve(49152, 8 * 512, BF16), wa.carve(49152 + 8192, 8 * 512, BF16)]]
    QT_OFF, KT_OFF, V_OFF = 65536, 98304, 106496
    qT = wa.carve(QT_OFF, 8 * S, BF16)
    kT = wa.carve(KT_OFF, 2 * S, BF16)
    vS = wa.carve(V_OFF, 16 * 256, BF16)
    qT3, kT3, vS3 = r3(qT.t, S), r3(kT.t, S), r3(vS.t, 256)

    def tk(base, h0, h1, t0, t1):
        out = []
        for h in range(h0, h1):
            o = base + h * 4096 + t0 * 2
            for tkk in wa.toks[o // wa.gran:(o + (t1 - t0) * 2 - 1) // wa.gran + 1]:
                if tkk not in out:
                    out.append(tkk)
        return out

    sets = []
    NSET = 3
    for s_ in range(NSET):
        b0 = s_ * 9216
        sqn = wa.carve(b0, 512, F32)
        d = dict(sq=sqn, qn=sqn, qg=wa.carve(b0 + 2048, 512, F32),
                 t1=wa.carve(b0 + 4096, 256, F32), t2=wa.carve(b0 + 5120, 256, F32),
                 t3=wa.carve(b0 + 6144, 256, F32), t4=wa.carve(b0 + 7168, 256, F32),
                 qr=wa.carve(b0 + 8192, 512, BF16))
        sets.append(d)
    _, b1t = statcols(c, "B1st", 16 * NSET)
    b1ring = [[Buf(b1t[:, s_ * 16 + 4 * q:s_ * 16 + 4 * q + 4], [Tok("b1s")]) for q in range(4)] for s_ in range(NSET)]

    def stat4(nh, sidx):
        return [Buf(b.t[:, 0:nh], b.k) for b in b1ring[sidx]]

    def process_qk(psb, col0, nh, gain, dstT3, dst_base, head0, i, sidx):
        w = sets[sidx]
        n = nh * 128
        psv = psb.t[:, col0:col0 + n]
        ss, rs, rs2, rstd = stat4(nh, sidx)
        sq, qn, qg, qr = w["sq"], w["qn"], w["qg"], w["qr"]
        p.op("act", I("activation", out=sq.t[:, 0:n], in_=psv, func=AF.Square), reads=psb.k, writes=sq.k)
        p.op("dve", I("tensor_reduce", out=ss.t, in_=r3(sq.t[:, 0:n], 128), axis=AX.X, op=ALU.add),
             reads=sq.k, writes=ss.k)
        p.op("dve", I("tensor_scalar", out=rs.t, in0=ss.t, scalar1=1.0 / 128, scalar2=EPS, op0=ALU.mult, op1=ALU.add),
             reads=ss.k, writes=rs.k)
        p.op("act", I("activation", out=rs2.t, in_=rs.t, func=AF.Sqrt), reads=rs.k, writes=rs2.k)
        p.op("dve", I("reciprocal", out=rstd.t, in_=rs2.t), reads=rs2.k, writes=rstd.k)
        p.op("dve", I("tensor_tensor", out=r3(qn.t[:, 0:n], 128), in0=r3(psv, 128), in1=bc_mid(rstd.t, 128), op=ALU.mult),
             reads=psb.k + rstd.k, writes=qn.k)
        p.op("dve", I("tensor_tensor", out=qg.t[:, 0:n], in0=qn.t[:, 0:n], in1=gain.t[:, 0:n], op=ALU.mult),
             reads=qn.k + gain.k, writes=qg.k)
        qg5 = qg.t[:, 0:n].rearrange("p (h g f e) -> p h g f e", g=2, f=2, e=32)
        qr5 = qr.t[:, 0:n].rearrange("p (h g f e) -> p h g f e", g=2, f=2, e=32)
        X1, X2 = qg5[:, :, :, 0, :], qg5[:, :, :, 1, :]
        Cb = cos.t[:, i, :].rearrange("p (g e) -> p g e", e=32).unsqueeze(1).broadcast_to([128, nh, 2, 32])
        Sb = sin.t[:, i, :].rearrange("p (g e) -> p g e", e=32).unsqueeze(1).broadcast_to([128, nh, 2, 32])

        def tv(b):
            return b.t[:, 0:nh * 64].rearrange("p (h g e) -> p h g e", g=2, e=32)
        t1, t2, t3, t4 = w["t1"], w["t2"], w["t3"], w["t4"]
        p.op("dve", I("tensor_tensor", out=tv(t1), in0=X1, in1=Cb, op=ALU.mult), reads=qg.k + cos.k, writes=t1.k)
        p.op("dve", I("tensor_tensor", out=tv(t2), in0=X2, in1=Sb, op=ALU.mult), reads=qg.k + sin.k, writes=t2.k)
        p.op("pool", I("tensor_tensor", out=tv(t3), in0=X1, in1=Sb, op=ALU.mult), reads=qg.k + sin.k, writes=t3.k)
        p.op("dve", I("tensor_tensor", out=tv(t4), in0=X2, in1=Cb, op=ALU.mult), reads=qg.k + cos.k, writes=t4.k)
        p.op("dve", I("tensor_tensor", out=qr5[:, :, :, 0, :], in0=tv(t1), in1=tv(t2), op=ALU.subtract),
             reads=t1.k + t2.k, writes=qr.k)
        p.op("dve", I("tensor_tensor", out=qr5[:, :, :, 1, :], in0=tv(t3), in1=tv(t4), op=ALU.add),
             reads=t3.k + t4.k, writes=qr.k)
        def part2():
            pb = c.ps[6 + (sidx % 2)]
            pT = pb.t.bitcast(BF16)
            p.op("pe", seq(*[I("transpose", out=pT[:, h * 128:(h + 1) * 128], in_=qr.t[:, h * 128:(h + 1) * 128],
                               identity=c.cidb.t) for h in range(nh)]),
                 reads=qr.k + c.cidb.k, writes=pb.k)
            p.op("act", I("activation", out=dstT3[:, head0:head0 + nh, i * 128:(i + 1) * 128], in_=r3(pT[:, 0:n], 128),
                          func=AF.Copy),
                 reads=pb.k, writes=tk(dst_base, head0, head0 + nh, i * 128, (i + 1) * 128))
        return part2

    cnt = 0
    pend = [None]
    for nb in range(3):
        wbh = WB[nb % 2]
        if nb == 0:
            load_w_block(c, c.w_in0, 0, 512, WB[0])
        load_w_block(c, c.w_in0, (nb + 1) * 512, 512, WB[(nb + 1) % 2])
        wv, wk = wb_views(wbh, 512)
        for i in range(NT):
            psb = c.ps[cnt % 2]
            p.op("pe", seq(*[I("matmul", psb.t, lhsT=hmT[:, ch, i * 128:(i + 1) * 128], rhs=wv(ch),
                               start=(ch == 0), stop=(ch == 15)) for ch in range(16)]),
                 reads=[hmk[i]] + wk, writes=psb.k)
            if pend[0] is not None:
                pend[0]()
            if nb < 2:
                pend[0] = process_qk(psb, 0, 4, qgb, qT3, QT_OFF, 4 * nb, i, cnt % NSET)
            else:
                pend[0] = process_qk(psb, 0, 2, kgb, kT3, KT_OFF, 0, i, cnt % NSET)
                vo = V_OFF + i * 512
                p.op("act", I("activation", out=vS3[:, i, :], in_=psb.t[:, 256:512], func=AF.Copy),
                     reads=psb.k, writes=wa.toks[vo // wa.gran:(vo + 511) // wa.gran + 1])
            cnt += 1

    pend[0]()
    stg = [wa.carve(0, 2048, F32), wa.carve(8192, 2048, F32)]
    dsts = [c.gA, c.xl, c.gL]
    pcnt = 0
    for jb in range(6):
        wbh = WB[(3 + jb) % 2]
        if jb > 0:
            load_w_block(c, c.w_in0, 1536 + jb * 512, 512, wbh)
        wv, wk = wb_views(wbh, 512)
        for q4 in range(4):
            j = jb * 4 + q4
            kind = j // 8
            sg = stg[j % 2]
            for tb in range(4):
                psb = c.ps[2 + (pcnt % 4)]
                pcnt += 1
                p.op("pe", seq(*[I("matmul", psb.t, lhsT=wv(ch)[:, q4 * 128:(q4 + 1) * 128],
                                   rhs=hmT[:, ch, tb * 512:(tb + 1) * 512], start=(ch == 0), stop=(ch == 15))
                                 for ch in range(16)]),
                     reads=hmk[4 * tb:4 * tb + 4] + wk, writes=psb.k)
                p.op("act", I("activation", out=sg.t[:, tb * 512:(tb + 1) * 512], in_=psb.t,
                              func=(AF.Copy if kind == 1 else AF.Silu)),
                     reads=psb.k, writes=sg.k)
            p.op("sp", I("dma_start", out=dsts[kind][(j % 8) * 128:(j % 8 + 1) * 128, :], in_=sg.t),
                 reads=sg.k, writes=[c.dtok(kind, j % 8)], dma=True)

    emit_attention(c, qT3, kT3, vS3, QT_OFF, KT_OFF, V_OFF, tk)
    emit_lru(c)
    emit_outproj(c, c.w_out0, c.x, c.x1, WB, final_gain=None)


def emit_attention(c, qT3, kT3, vS3, QT_OFF, KT_OFF, V_OFF, tk):
    p, wa = c.p, c.wa
    hmT, hmk = c.hmT, c.hmk
    SCALE = 128 ** -0.5
    gAb = [wa.carve(0, 2048, F32), wa.carve(8192, 2048, F32)]
    NP = 4
    PT = [wa.carve(16384 + 1024 * q, 512, BF16) for q in range(NP)]
    rec = [wa.carve(20480 + 2048 * q, 512, F32) for q in range(2)]
    ob = [wa.carve(24576 + 2048 * q, 512, F32) for q in range(2)]
    sT = [c.ps[0], c.ps[1], c.ps[6], c.ps[7]]
    psO = [c.ps[2], c.ps[3]]
    psS = [c.ps[4], c.ps[5]]
    vtoks = wa.toks[V_OFF // wa.gran:(V_OFF + 8191) // wa.gran + 1]
    blk = 0
    ptc = 0
    for h in range(8):
        kvh = h // 4
        g_ = gAb[h % 2]
        p.op("sp", I("dma_start", out=g_.t, in_=c.gA[h * 128:(h + 1) * 128, :]),
             reads=[c.dtok(0, h)], writes=g_.k, dma=True)
        for qb in range(4):
            O, SM = psO[blk % 2], psS[blk % 2]
            qtok = tk(QT_OFF, h, h + 1, qb * 512, (qb + 1) * 512)
            ktok = tk(KT_OFF, kvh, kvh + 1, 0, S)

            def emit_S(st):
                b = sT[st % 4]
                p.op("pe", I("matmul", b.t, lhsT=kT3[:, kvh, st * 128:(st + 1) * 128],
                             rhs=qT3[:, h, qb * 512:(qb + 1) * 512], start=True, stop=True),
                     reads=qtok + ktok, writes=b.k)

            def emit_E(st, pt):
                b = sT[st % 4]
                p.op("act", I("activation", out=pt.t, in_=b.t, func=AF.Exp, scale=SCALE), reads=b.k, writes=pt.k)

            def emit_PV(st, pt):
                p.op("pe", seq(I("matmul", O.t, lhsT=vS3[:, st, kvh * 128:(kvh + 1) * 128], rhs=pt.t,
                                 start=(st == 0), stop=(st == 15)),
                               I("matmul", SM.t, lhsT=c.onesb.t, rhs=pt.t, start=(st == 0), stop=(st == 15))),
                     reads=pt.k + vtoks + c.onesb.k, writes=O.k + SM.k)
            pts = {}
            emit_S(0)
            emit_S(1)
            emit_S(2)
            for st in range(16):
                pts[st] = PT[ptc % NP]
                ptc += 1
                emit_E(st, pts[st])
                if st + 3 < 16:
                    emit_S(st + 3)
                emit_PV(st, pts[st])
            r_, o_ = rec[blk % 2], ob[blk % 2]
            p.op("dve", I("reciprocal", out=r_.t, in_=SM.t), reads=SM.k, writes=r_.k)
            p.op("dve", I("tensor_tensor", out=o_.t, in0=O.t, in1=r_.t, op=ALU.mult), reads=O.k + r_.k, writes=o_.k)
            p.op("pool", I("tensor_tensor", out=hmT[:, h, qb * 512:(qb + 1) * 512], in0=o_.t,
                           in1=g_.t[:, qb * 512:(qb + 1) * 512], op=ALU.mult),
                 reads=o_.k + g_.k, writes=hmk[4 * qb:4 * qb + 4])
            blk += 1


def emit_lru(c):
    p, wa = c.p, c.wa
    hmT, hmk = c.hmT, c.hmk
    sbt = c.sbt
    cw = sbt("cw_s", [128, 8, 4], F32)
    cb = sbt("cb_s", [128, 8], F32)
    lba = sbt("lba_s", [128, 16], F32)
    lbx = sbt("lbx_s", [128, 16], F32)
    lam = sbt("lam_s", [128, 16], F32)
    for b, src in ((cw, c.cw), (cb, c.cb), (lba, c.lba), (lbx, c.lbx), (lam, c.llam)):
        p.op("sp", I("dma_start", out=b.t, in_=src), writes=b.k, dma=True)
    wa_sb = wa.carve(118784, 16 * 128, BF16)
    wx_sb = wa.carve(122880, 16 * 128, BF16)
    p.op("pool", I("dma_start", out=r3(wa_sb.t, 128), in_=c.lru_wa.rearrange("r n c d -> c (r n) d")),
         writes=wa_sb.k, dma=True)
    p.op("pool", I("dma_start", out=r3(wx_sb.t, 128), in_=c.lru_wx.rearrange("r n c d -> c (r n) d")),
         writes=wx_sb.k, dma=True)
    wa3, wx3 = r3(wa_sb.t, 128), r3(wx_sb.t, 128)
    e_ = sbt("l_e", [128, 16], F32)
    l_ = sbt("l_l", [128, 16], F32)
    u_ = sbt("l_u", [128, 16], F32)
    m_ = sbt("l_m", [128, 16], F32)
    sc4 = sbt("l_sc4", [128, 16], F32)
    p.op("act", I("activation", out=e_.t, in_=lam.t, func=AF.Exp, scale=-1.0), reads=lam.k, writes=e_.k)
    p.op("act", I("activation", out=l_.t, in_=e_.t, func=AF.Ln, bias=1.0), reads=e_.k, writes=l_.k)
    p.op("dve", I("tensor_scalar", out=u_.t, in0=e_.t, scalar1=-1.0 / 3, scalar2=0.5, op0=ALU.mult, op1=ALU.add),
         reads=e_.k, writes=u_.k)
    p.op("dve", I("tensor_tensor", out=u_.t, in0=u_.t, in1=e_.t, op=ALU.mult), reads=u_.k + e_.k, writes=u_.k)
    p.op("dve", I("tensor_scalar", out=u_.t, in0=u_.t, scalar1=-1.0, scalar2=1.0, op0=ALU.mult, op1=ALU.add),
         reads=u_.k, writes=u_.k)
    p.op("dve", I("tensor_tensor", out=u_.t, in0=u_.t, in1=e_.t, op=ALU.mult), reads=u_.k + e_.k, writes=u_.k)
    p.op("dve", I("tensor_single_scalar", out=m_.t, in_=e_.t, scalar=0.05, op=ALU.is_lt), reads=e_.k, writes=m_.k)
    p.op("dve", I("tensor_tensor", out=u_.t, in0=u_.t, in1=l_.t, op=ALU.subtract), reads=u_.k + l_.k, writes=u_.k)
    p.op("dve", I("tensor_tensor", out=u_.t, in0=u_.t, in1=m_.t, op=ALU.mult), reads=u_.k + m_.k, writes=u_.k)
    p.op("dve", I("tensor_tensor", out=u_.t, in0=u_.t, in1=l_.t, op=ALU.add), reads=u_.k + l_.k, writes=u_.k)
    p.op("dve", I("tensor_scalar", out=sc4.t, in0=u_.t, scalar1=4.0, scalar2=None, op0=ALU.mult), reads=u_.k, writes=sc4.k)

    p2 = sbt("l_p2", [128, 16], F32)
    n2 = sbt("l_n2", [128, 16], F32)
    p.op("dve", I("tensor_scalar", out=p2.t, in0=sc4.t, scalar1=2.0, scalar2=None, op0=ALU.mult), reads=sc4.k, writes=p2.k)
    p.op("dve", I("tensor_scalar", out=n2.t, in0=sc4.t, scalar1=-2.0, scalar2=None, op0=ALU.mult), reads=sc4.k, writes=n2.k)
    n4 = sbt("l_n4", [128, 16], F32)
    p.op("dve", I("tensor_scalar", out=n4.t, in0=sc4.t, scalar1=-4.0, scalar2=None, op0=ALU.mult), reads=sc4.k, writes=n4.k)
    K8 = 8192
    X = [wa.carve(0, 2048, F32), wa.carve(K8, 2048, F32)]
    GL = wa.carve(2 * K8, 2048, F32)
    XC = wa.carve(3 * K8, 2048, F32)
    Rb = [wa.carve(4 * K8, 2048, F32), wa.carve(5 * K8, 2048, F32)]
    Ab = [wa.carve(6 * K8, 2048, F32), wa.carve(7 * K8, 2048, F32)]
    Pb = [wa.carve(8 * K8, 2048, F32), wa.carve(9 * K8, 2048, F32)]
    Ibb = [wa.carve(10 * K8, 2048, F32), wa.carve(11 * K8, 2048, F32)]
    H = wa.carve(12 * K8, 2048, F32)
    Y = wa.carve(13 * K8, 2048, F32)
    XCB = wa.carve(14 * K8, 2048, BF16)
    pc = 0

    def load(j):
        p.op("sp", I("dma_start", out=X[j % 2].t, in_=c.xl[j * 128:(j + 1) * 128, :]), reads=[c.dtok(1, j)],
             writes=X[j % 2].k, dma=True)
    pcn = [0]

    def gate_mm(j, dr, w3, bias, dst):
        gi = dr * 8 + j
        for tb in range(4):
            psb = c.ps[pcn[0] % 6]
            pcn[0] += 1
            p.op("pe", I("matmul", psb.t, lhsT=w3[:, gi, :], rhs=XCB.t[:, tb * 512:(tb + 1) * 512],
                         start=True, stop=True), reads=XCB.k + wa_sb.k + wx_sb.k, writes=psb.k)
            p.op("act", I("activation", out=dst.t[:, tb * 512:(tb + 1) * 512], in_=psb.t, func=AF.Sigmoid,
                          bias=bias.t[:, gi:gi + 1]), reads=psb.k + bias.k, writes=dst.k)

    def head(j):
        x_ = X[j % 2]
        if j + 1 < 8:
            load(j + 1)
        p.op("dve", I("tensor_scalar", out=XC.t, in0=x_.t, scalar1=cw.t[:, j, 2:3], scalar2=cb.t[:, j:j + 1],
                      op0=ALU.mult, op1=ALU.add), reads=x_.k + cw.k + cb.k, writes=XC.k)
        p.op("dve", I("scalar_tensor_tensor", out=XC.t[:, 2:S], in0=x_.t[:, 0:S - 2], scalar=cw.t[:, j, 0:1],
                      in1=XC.t[:, 2:S], op0=ALU.mult, op1=ALU.add), reads=x_.k + XC.k, writes=XC.k)
        p.op("dve", I("scalar_tensor_tensor", out=XC.t[:, 1:S], in0=x_.t[:, 0:S - 1], scalar=cw.t[:, j, 1:2],
                      in1=XC.t[:, 1:S], op0=ALU.mult, op1=ALU.add), reads=x_.k + XC.k, writes=XC.k)
        p.op("dve", I("scalar_tensor_tensor", out=XC.t[:, 0:S - 1], in0=x_.t[:, 1:S], scalar=cw.t[:, j, 3:4],
                      in1=XC.t[:, 0:S - 1], op0=ALU.mult, op1=ALU.add), reads=x_.k + XC.k, writes=XC.k)
        p.op("act", I("activation", out=XCB.t, in_=XC.t, func=AF.Copy), reads=XC.k, writes=XCB.k)
        for dr in range(2):
            gate_mm(j, dr, wa3, lba, Rb[dr])

    def mid(j):
        p.op("sp", I("dma_start", out=GL.t, in_=c.gL[j * 128:(j + 1) * 128, :]), reads=[c.dtok(2, j)], writes=GL.k, dma=True)
        for dr in range(2):
            gate_mm(j, dr, wx3, lbx, Ibb[dr])
        for dr in range(2):
            gi = dr * 8 + j
            R, A, P_, Ib = Rb[dr], Ab[dr], Pb[dr], Ibb[dr]
            p.op("act", I("activation", out=A.t, in_=R.t, func=AF.Exp, scale=n2.t[:, gi:gi + 1]), reads=R.k + n2.k, writes=A.k)
            p.op("act", I("activation", out=P_.t, in_=R.t, func=AF.Exp, scale=n4.t[:, gi:gi + 1]), reads=R.k + n4.k, writes=P_.k)
            p.op("act", I("activation", out=R.t, in_=R.t, func=AF.Tanh, scale=p2.t[:, gi:gi + 1]), reads=R.k + p2.k, writes=R.k)
            p.op("pool", I("tensor_tensor", out=Ib.t, in0=Ib.t, in1=XC.t, op=ALU.mult), reads=Ib.k + XC.k, writes=Ib.k)
            p.op("dve", I("scalar_tensor_tensor", out=P_.t, in0=P_.t, scalar=1.0, in1=R.t, op0=ALU.add, op1=ALU.mult),
                 reads=P_.k + R.k, writes=P_.k)

    def tailA(j):
        for dr in range(2):
            P_ = Pb[dr]
            p.op("act", I("activation", out=P_.t, in_=P_.t, func=AF.Sqrt), reads=P_.k, writes=P_.k)

    def tail(j):
        for dr in range(2):
            R, A, P_, Ib = Rb[dr], Ab[dr], Pb[dr], Ibb[dr]
            p.op("dve", I("tensor_tensor", out=Ib.t, in0=Ib.t, in1=P_.t, op=ALU.mult), reads=Ib.k + P_.k, writes=Ib.k)
            if dr == 0:
                p.op("dve", I("tensor_tensor_scan", out=Y.t, data0=A.t, data1=Ib.t, initial=0.0,
                              op0=ALU.mult, op1=ALU.add), reads=A.k + Ib.k, writes=Y.k)
            else:
                p.op("dve", I("tensor_tensor_scan", out=H.t[:, ::-1], data0=A.t[:, ::-1], data1=Ib.t[:, ::-1],
                              initial=0.0, op0=ALU.mult, op1=ALU.add), reads=A.k + Ib.k, writes=H.k)
        p.op("pool", I("tensor_tensor", out=Y.t, in0=Y.t, in1=H.t, op=ALU.add), reads=Y.k + H.k, writes=Y.k)
        p.op("pool", I("tensor_tensor", out=hmT[:, 8 + j, :], in0=Y.t, in1=GL.t, op=ALU.mult),
             reads=Y.k + GL.k, writes=hmk)

    load(0)
    head(0)
    for j in range(8):
        mid(j)
        tailA(j)
        if j + 1 < 8:
            head(j + 1)
        tail(j)


def emit_outproj(c, w_out, x_src, x_dst, WB, final_gain=None):
    p, wa = c.p, c.wa
    hmT, hmk = c.hmT, c.hmk
    NB_ = 4
    xin = [wa.carve(2048 * q, 512, F32) for q in range(NB_)]
    xo = [wa.carve(8192 + 2048 * q, 512, F32) for q in range(NB_)]
    jobs = [(nb, i) for nb in range(4) for i in range(NT)]

    def load(k):
        nb, i = jobs[k]
        xi = xin[k % NB_]
        p.op("sp", I("dma_start", out=xi.t, in_=x_src[i * 128:(i + 1) * 128, nb * 512:(nb + 1) * 512]),
             writes=xi.k, dma=True)
    for k in range(min(3, len(jobs))):
        load(k)
    wv = wk = None
    for k, (nb, i) in enumerate(jobs):
        if i == 0:
            if nb == 0:
                load_w_block(c, w_out, 0, 512, WB[0])
            if nb + 1 < 4:
                load_w_block(c, w_out, (nb + 1) * 512, 512, WB[(nb + 1) % 2])
            wv, wk = wb_views(WB[nb % 2], 512)
        if k + 3 < len(jobs):
            load(k + 3)
        psb = c.ps[k % 4]
        xi, xo_ = xin[k % NB_], xo[k % NB_]
        p.op("pe", seq(*[I("matmul", psb.t, lhsT=hmT[:, ch, i * 128:(i + 1) * 128], rhs=wv(ch),
                           start=(ch == 0), stop=(ch == 15)) for ch in range(16)]),
             reads=[hmk[i]] + wk, writes=psb.k)
        p.op("dve", I("tensor_tensor", out=xo_.t, in0=psb.t, in1=xi.t, op=ALU.add),
             reads=psb.k + xi.k, writes=xo_.k)
        p.op("pool", I("dma_start", out=x_dst[i * 128:(i + 1) * 128, nb * 512:(nb + 1) * 512], in_=xo_.t),
             reads=xo_.k, writes=[c.dtok("x1", i, nb)], dma=True)


def host_consts():
    inv = (10000.0 ** (-np.arange(0, 64, 2, dtype=np.float32) / 64)).astype(np.float32)
    t = np.arange(S)
    row = (t // 64).astype(np.float32)
    col = (t % 64).astype(np.float32)
    ar = row[:, None] * inv[None, :]
    ac = col[:, None] * inv[None, :]
    cosT = np.concatenate([np.cos(ar), np.cos(ac)], axis=1).astype(np.float32)
    sinT = np.concatenate([np.sin(ar), np.sin(ac)], axis=1).astype(np.float32)
    identb = np.eye(128, dtype=np.float32).astype(ml_dtypes.bfloat16)
    identf = np.eye(128, dtype=np.float32)
    sg, ta = np.meshgrid(np.arange(128), np.arange(128), indexing="ij")
    mneg = np.where(sg <= ta, 0.0, -1e30).astype(np.float32)
    return dict(cosT=cosT, sinT=sinT, identb=identb, identf=identf, mneg=mneg)


def shared_inputs(inp, mode):
    f = np.ascontiguousarray
    hc = host_consts()
    m = dict(identb=hc["identb"], identf=hc["identf"])
    if mode in ("l0", "fused"):
        m.update(cosT=hc["cosT"], sinT=hc["sinT"],
                 gT0=f(inp["norm_gain"][0].reshape(16, 128).T),
                 w_in0=f(inp["even_w_in"][0]), w_out0=f(inp["even_w_out"][0]),
                 q_gain=f(inp["q_norm_gain"][0].reshape(1, 128)), k_gain=f(inp["k_norm_gain"][0].reshape(1, 128)),
                 cw=f(inp["conv_w"][0].reshape(4, 8, 128).transpose(2, 1, 0)),
                 cb=f(inp["conv_b"][0].reshape(8, 128).T),
                 lru_wa=f(inp["lru_wa"][0]), lru_wx=f(inp["lru_wx"][0]),
                 lba=f(inp["lru_ba"][0].reshape(2, 8, 128).transpose(2, 0, 1).reshape(128, 16)),
                 lbx=f(inp["lru_bx"][0].reshape(2, 8, 128).transpose(2, 0, 1).reshape(128, 16)),
                 llam=f(inp["lru_lambda"][0].reshape(2, 8, 128).transpose(2, 0, 1).reshape(128, 16)))
    if mode in ("l1", "fused"):
        m.update(gT1=f(inp["norm_gain"][1].reshape(16, 128).T),
                 w_in1=f(inp["odd_w_in"][0]), w_out1=f(inp["odd_w_out"][0]),
                 gate_bias=f(inp["odd_gate_bias"][0].reshape(32, 1)),
                 onorm=f(inp["odd_norm_gain"][0].reshape(1, D)),
                 fgain=f(inp["final_gain"].reshape(1, D)),
                 mnegf=hc["mneg"], mnegb=np.ascontiguousarray(hc["mneg"].T))
    return m


_CACHE = {}


def run_mode(mode, inp, x_full, debug=False):
    key = (mode, debug)
    if key not in _CACHE:
        _CACHE[key] = build_program(mode, debug)
    nc, c = _CACHE[key]
    sh = shared_inputs(inp, mode)
    in_maps = []
    for b in range(8):
        m = dict(sh)
        m["x"] = np.ascontiguousarray(x_full[b])
        in_maps.append(m)
    res = run_bass_kernel_spmd(nc, in_maps, core_ids=list(range(8)))
    return res.results


def kernel(**inputs):
    inp = {k: np.asarray(v) for k, v in inputs.items()}
    x = inp["x"].astype(np.float32, copy=False)
    r = run_mode("fused", inp, x)
    return np.stack([r[b]["out"] for b in range(8)], axis=0).astype(np.float32)


def emit_layer1(c):
    nc, p, wa = c.nc, c.p, c.wa
    if not hasattr(c, "hmT"):
        c.hmT = r3(c.hm.t[:, :], S)
        c.hmk = [Tok("hm%d" % i) for i in range(NT)]
    hmT, hmk = c.hmT, c.hmk
    sbt = c.sbt
    gT1 = sbt("gT1s", [128, 16], F32)
    gbias = sbt("gbias", [32, 1], F32)
    mnf = sbt("mnf", [128, 128], F32)
    mnb = sbt("mnb", [128, 128], F32)
    p.op("sp", I("dma_start", out=gT1.t, in_=c.gT1), writes=gT1.k, dma=True)
    p.op("sp", I("dma_start", out=gbias.t, in_=c.gate_bias), writes=gbias.k, dma=True)
    p.op("sp", I("dma_start", out=mnf.t, in_=c.mnegf), writes=mnf.k, dma=True)
    p.op("sp", I("dma_start", out=mnb.t, in_=c.mnegb), writes=mnb.k, dma=True)

    fused = ("x1", 0, 0) in c.dtoks
    phase_A(c, c.x1, gT1, "1", src_toks=(lambda i: [c.dtok("x1", i, nb) for nb in range(4)]) if fused else None)

    WB = [[wa.carve(32768, 8 * 512, BF16), wa.carve(32768 + 8192, 8 * 512, BF16)],
          [wa.carve(49152, 8 * 512, BF16), wa.carve(49152 + 8192, 8 * 512, BF16)]]
    wbi = 0
    wg = wa.carve(16384, 16 * 32, BF16)
    p.op("pool", I("dma_start", out=r3(wg.t, 32), in_=c.w_in1[:, 8192:8224].rearrange("(c p) n -> p c n", p=128)),
         writes=wg.k, dma=True)
    wg3 = r3(wg.t, 32)
    Gt = wa.carve(20480, 2048, F32, parts=32)
    for tb in range(4):
        psb = c.ps[2 + tb]
        p.op("pe", seq(*[I("matmul", psb.t[0:32, :], lhsT=wg3[:, ch, :], rhs=hmT[:, ch, tb * 512:(tb + 1) * 512],
                           start=(ch == 0), stop=(ch == 15)) for ch in range(16)]),
             reads=hmk[4 * tb:4 * tb + 4] + wg.k, writes=psb.k)
        p.op("act", I("activation", out=Gt.t[:, tb * 512:(tb + 1) * 512], in_=psb.t[0:32, :], func=AF.Identity,
                      bias=gbias.t), reads=psb.k + gbias.k, writes=Gt.k)
    p.op("sp", I("dma_start", out=c.gsc, in_=Gt.t), reads=Gt.k, writes=[c.dtok("gsc")], dma=True)

    gates_gen = emit_gates(c)

    def gstep(n=1):
        for _ in range(n):
            next(gates_gen, None)

    stgT = [wa.carve(0, 2048, BF16), wa.carve(4096, 2048, BF16)]
    pc = 0
    for jb in range(4):
        wbh = WB[wbi % 2]
        wbi += 1
        load_w_block(c, c.w_in1, jb * 512, 512, wbh)
        wv, wk = wb_views(wbh, 512)
        for q4 in range(4):
            j = jb * 4 + q4
            sg = stgT[j % 2]
            dst = c.qT1 if j < 8 else c.kT1
            scl = 1.0 if j < 8 else 128 ** -0.5
            for tb in range(4):
                psb = c.ps[2 + (pc % 4)]
                pc += 1
                p.op("pe", seq(*[I("matmul", psb.t, lhsT=wv(ch)[:, q4 * 128:(q4 + 1) * 128],
                                   rhs=hmT[:, ch, tb * 512:(tb + 1) * 512], start=(ch == 0), stop=(ch == 15))
                                 for ch in range(16)]),
                     reads=hmk[4 * tb:4 * tb + 4] + wk, writes=psb.k)
                p.op("act", I("activation", out=sg.t[:, tb * 512:(tb + 1) * 512], in_=psb.t, func=AF.Copy, scale=scl),
                     reads=psb.k, writes=sg.k)
                gstep(1)
            p.op("sp", I("dma_start", out=bass.AP(dst.tensor, (j % 8) * 128, [[1024, 128], [128 * 1024, 16], [1, 128]]),
                         in_=r3(sg.t, 128)), reads=sg.k, writes=[c.dtok("qk", j)], dma=True)
    onb1 = wa.carve(24576, 2048, F32)
    p.op("sp", I("dma_start", out=onb1.t, in_=c.onorm.broadcast_to([128, D])), writes=onb1.k, dma=True)
    stg = [wa.carve(8192 + 2048 * q, 512, F32) for q in range(4)]
    blocks = []
    for b in range(2):
        blocks.append(("k", 1024 + b * 512, c.k1, b * 512))
    for b in range(4):
        blocks.append(("v", 2048 + b * 512, c.v1, b * 512))
    for b in range(4):
        blocks.append(("o", 4096 + b * 512, c.osig, b * 512))
    for b in range(4):
        blocks.append(("z", 6144 + b * 512, c.zsil, b * 512))
    cnt = 0
    for (kind, col0, dst, dc0) in blocks:
        wbh = WB[wbi % 2]
        wbi += 1
        load_w_block(c, c.w_in1, col0, 512, wbh)
        wv, wk = wb_views(wbh, 512)
        for i in range(NT):
            psb = c.ps[cnt % 2]
            sg = stg[cnt % 4]
            cnt += 1
            p.op("pe", seq(*[I("matmul", psb.t, lhsT=hmT[:, ch, i * 128:(i + 1) * 128], rhs=wv(ch),
                               start=(ch == 0), stop=(ch == 15)) for ch in range(16)]),
                 reads=[hmk[i]] + wk, writes=psb.k)
            if kind in ("k", "v"):
                o_ = sg.t.bitcast(BF16)[:, 0:512]
                if kind == "k":
                    p.op("act", I("activation", out=o_, in_=psb.t, func=AF.Copy, scale=128 ** -0.5),
                         reads=psb.k, writes=sg.k)
                else:
                    p.op("dve", I("tensor_copy", out=o_, in_=psb.t), reads=psb.k, writes=sg.k)
            else:
                o_ = sg.t
                p.op("act", I("activation", out=o_, in_=psb.t, func=(AF.Sigmoid if kind == "o" else AF.Silu)),
                     reads=psb.k, writes=sg.k)
                if kind == "z":
                    p.op("dve", I("tensor_tensor", out=o_, in0=o_, in1=onb1.t[:, dc0:dc0 + 512], op=ALU.mult),
                         reads=sg.k + onb1.k, writes=sg.k)
            p.op("sp", I("dma_start", out=dst[i * 128:(i + 1) * 128, dc0:dc0 + 512], in_=o_),
                 reads=sg.k, writes=[c.dtok(kind, i, dc0)], dma=True)
            gstep(1)
    for nb in range(4):
        for half in range(2):
            src = c.w_out1[half * 1024:(half + 1) * 1024, nb * 512:(nb + 1) * 512].rearrange("(c p) n -> p c n", p=128)
            p.op("pool", I("dma_start", out=hmT[:, half * 8:(half + 1) * 8, nb * 512:(nb + 1) * 512], in_=src),
                 writes=hmk + [c.dtok("wout1", nb, half)], dma=True)
    stop = getattr(c, "stop", None)
    if stop == "B":
        return
    if stop == "BF":
        emit_final(c)
        return
    for _ in gates_gen:
        pass
    if stop == "G":
        return
    emit_mlstm(c, mnf, mnb)
    if stop == "M":
        return
    emit_final(c)


def emit_gates(c):
    nc, p, wa = c.nc, c.p, c.wa
    K8 = 8192
    names = ["IP", "FP", "ONES", "T1", "T2", "F", "G", "PM", "NPM", "INTER", "WS"]
    B = {n: wa.carve(65536 + i * K8, 2048, F32, parts=40) for i, n in enumerate(names[:8])}
    B["NPM"], B["INTER"], B["WS"] = B["IP"], B["FP"], B["ONES"]
    gs = c.dtok("gsc")
    for n in ("IP", "FP"):
        p.op("pool", I("memset", B[n].t, 0.0), writes=B[n].k)
        yield
    p.op("pool", I("memset", B["ONES"].t, 1.0), writes=B["ONES"].k)
    yield
    p.op("sp", I("dma_start", out=B["IP"].t[0:8, :], in_=c.gsc[0:8, :]), reads=[gs], writes=B["IP"].k, dma=True)
    yield
    p.op("sp", I("dma_start", out=B["IP"].t[32:40, :], in_=c.gsc[8:16, :]), reads=[gs], writes=B["IP"].k, dma=True)
    yield
    p.op("sp", I("dma_start", out=B["FP"].t[0:8, :], in_=c.gsc[16:24, :]), reads=[gs], writes=B["FP"].k, dma=True)
    yield
    p.op("sp", I("dma_start", out=B["FP"].t[32:40, :], in_=c.gsc[24:32, :]), reads=[gs], writes=B["FP"].k, dma=True)
    yield
    IP, FP, ONES, T1, T2, F_, G_, PM, NPM, INTER, WS = [B[n] for n in names]

    def dve(fn, r, w):
        p.op("dve", fn, reads=[t for b in r for t in b.k], writes=[t for b in w for t in b.k])

    def act(fn, r, w):
        p.op("act", fn, reads=[t for b in r for t in b.k], writes=[t for b in w for t in b.k])
    act(I("activation", out=T1.t, in_=FP.t, func=AF.Abs), [FP], [T1])
    yield
    act(I("activation", out=T1.t, in_=T1.t, func=AF.Exp, scale=-1.0), [T1], [T1])
    yield
    act(I("activation", out=T1.t, in_=T1.t, func=AF.Ln, bias=1.0), [T1], [T1])
    yield
    dve(I("tensor_single_scalar", out=T2.t, in_=FP.t, scalar=0.0, op=ALU.min), [FP], [T2])
    yield
    dve(I("tensor_tensor", out=T2.t, in0=T2.t, in1=T1.t, op=ALU.subtract), [T2, T1], [T2])
    yield
    dve(I("tensor_tensor_scan", out=F_.t[0:8, :], data0=ONES.t[0:8, :], data1=T2.t[0:8, :], initial=0.0,
          op0=ALU.mult, op1=ALU.add), [ONES, T2], [F_])
    yield
    dve(I("tensor_tensor_scan", out=F_.t[32:40, ::-1], data0=ONES.t[32:40, ::-1], data1=T2.t[32:40, ::-1], initial=0.0,
          op0=ALU.mult, op1=ALU.add), [ONES, T2], [F_])
    yield
    dve(I("tensor_tensor", out=G_.t[0:8, :], in0=IP.t[0:8, :], in1=F_.t[0:8, :], op=ALU.subtract), [IP, F_], [G_])
    yield
    dve(I("tensor_tensor", out=G_.t[32:40, :], in0=IP.t[32:40, :], in1=F_.t[32:40, :], op=ALU.subtract), [IP, F_], [G_])
    yield
    dve(I("tensor_tensor_scan", out=PM.t[0:8, :], data0=G_.t[0:8, :], data1=G_.t[0:8, :], initial=0.0,
          op0=ALU.max, op1=ALU.max), [G_], [PM])
    yield
    dve(I("tensor_tensor_scan", out=PM.t[32:40, ::-1], data0=G_.t[32:40, ::-1], data1=G_.t[32:40, ::-1], initial=0.0,
          op0=ALU.max, op1=ALU.max), [G_], [PM])
    yield
    for r0 in (0, 32):
        dve(I("tensor_tensor", out=T1.t[r0:r0 + 8, :], in0=F_.t[r0:r0 + 8, :], in1=PM.t[r0:r0 + 8, :], op=ALU.add),
            [F_, PM], [T1])
        yield
        act(I("activation", out=T1.t[r0:r0 + 8, :], in_=T1.t[r0:r0 + 8, :], func=AF.Exp, scale=-1.0), [T1], [T1])
        yield
        p.op("pool", I("tensor_scalar", out=NPM.t[r0:r0 + 8, :], in0=PM.t[r0:r0 + 8, :], scalar1=-1.0, scalar2=None,
                       op0=ALU.mult), reads=PM.k, writes=NPM.k)
        yield
    EN = T1
    pme = c.sbt("g_pme", [40, 16], F32)
    pms = c.sbt("g_pms", [40, 16], F32)
    dec = c.sbt("g_dec", [40, 16], F32)
    PM3 = r3(PM.t, 128)
    G3 = r3(G_.t, 128)
    p.op("pool", I("memset", pms.t, 0.0), writes=pms.k)
    yield
    dve(I("tensor_copy", out=pme.t[0:8, :], in_=PM3[0:8, :, 127]), [PM], [pme])
    yield
    dve(I("tensor_copy", out=pme.t[32:40, :], in_=PM3[32:40, :, 0]), [PM], [pme])
    yield
    dve(I("tensor_copy", out=pms.t[0:8, 1:16], in_=pme.t[0:8, 0:15]), [pme], [pms])
    yield
    dve(I("tensor_copy", out=pms.t[32:40, 0:15], in_=pme.t[32:40, 1:16]), [pme], [pms])
    yield
    for r0 in (0, 32):
        sl = slice(r0, r0 + 8)
        dve(I("tensor_tensor", out=r3(T2.t, 128)[sl], in0=PM3[sl], in1=bc_mid(pms.t[sl, :], 128), op=ALU.subtract),
            [PM, pms], [T2])
        yield
        act(I("activation", out=INTER.t[sl, :], in_=T2.t[sl, :], func=AF.Exp, scale=-1.0), [T2], [INTER])
        yield
        dve(I("tensor_tensor", out=r3(T2.t, 128)[sl], in0=G3[sl], in1=bc_mid(pme.t[sl, :], 128), op=ALU.subtract),
            [G_, pme], [T2])
        yield
        act(I("activation", out=WS.t[sl, :], in_=T2.t[sl, :], func=AF.Exp), [T2], [WS])
        yield
        dve(I("tensor_tensor", out=dec.t[sl, :], in0=pms.t[sl, :], in1=pme.t[sl, :], op=ALU.subtract), [pms, pme], [dec])
        yield
        act(I("activation", out=dec.t[sl, :], in_=dec.t[sl, :], func=AF.Exp), [dec], [dec])
        yield
    for d, r0 in ((0, 0), (1, 32)):
        p.op("sp", I("dma_start", out=bass.AP(c.npm_d.tensor, d * 16 * 1024, [[128, 8], [1024, 16], [1, 128]]),
                     in_=r3(NPM.t, 128)[r0:r0 + 8]), reads=NPM.k, writes=[c.dtok("npm", d)], dma=True)
        yield
        p.op("sp", I("dma_start", out=bass.AP(c.inter_d.tensor, d * 16 * 1024, [[128, 8], [1024, 16], [1, 128]]),
                     in_=r3(INTER.t, 128)[r0:r0 + 8]), reads=INTER.k, writes=[c.dtok("inter", d)], dma=True)
        yield
        p.op("sp", I("dma_start", out=c.dec_d[0:1, d * 128:(d + 1) * 128].rearrange("o (h c) -> (o h) c", c=16),
                     in_=dec.t[r0:r0 + 8, :]), reads=dec.k, writes=[c.dtok("dec")], dma=True)
        yield
    c.Gcol = c.sbt("Gcol", [128, 256], F32)
    c.WScol = c.sbt("WScol", [128, 256], F32)
    c.ENcol = c.sbt("ENcol", [128, 256], F32)
    c.decbc = c.sbt("decbc", [128, 256], F32)
    p.op("sp", I("dma_start", out=c.decbc.t, in_=c.dec_d.broadcast_to([128, 256])),
         reads=[c.dtok("dec")], writes=c.decbc.k, dma=True)
    yield
    k = 0
    for src, dstc in ((G_, c.Gcol), (WS, c.WScol), (EN, c.ENcol)):
        for d, r0 in ((0, 0), (1, 32)):
            psb = c.ps[k % 2]
            k += 1
            p.op("pe", seq(*[I("transpose", out=psb.t[:, ch * 8:(ch + 1) * 8], in_=src.t[r0:r0 + 8, ch * 128:(ch + 1) * 128],
                               identity=c.cidf.t[r0:r0 + 8, r0:r0 + 8]) for ch in range(16)]),
                 reads=src.k + c.cidf.k, writes=psb.k)
            yield
            p.op("act", I("activation", out=dstc.t[:, d * 128:(d + 1) * 128], in_=psb.t[:, 0:128], func=AF.Copy),
                 reads=psb.k, writes=dstc.k)
            yield
    if c.debug:
        for nm, b in (("dGcol", c.Gcol), ("dWScol", c.WScol), ("dENcol", c.ENcol), ("ddecbc", c.decbc)):
            ap = nc.dram_tensor(nm, [128, 256], F32, kind="ExternalOutput").ap()
            p.op("sp", I("dma_start", out=ap, in_=b.t), reads=b.k, dma=True)
            yield


def emit_mlstm(c, mnf, mnb):
    nc, p, wa = c.nc, c.p, c.wa
    hmT, hmk = c.hmT, c.hmk
    NV = 258
    NVF = 320
    NVB = 384
    C32 = [wa.private(h * NVF * 4, NV, F32) for h in range(8)]
    o0 = 8 * NVF * 4
    Cb = [wa.private(o0 + h * NVB * 2, NV, BF16) for h in range(8)]
    o1 = o0 + 8 * NVB * 2
    SETB = 18688
    LD = []
    for s_ in range(3):
        b = o1 + s_ * SETB
        LD.append(dict(qT=wa.carve(b, 1024, BF16), kT=wa.carve(b + 2048, 1024, BF16), k=wa.carve(b + 4096, 1024, BF16),
                       V=wa.carve(b + 6144, 8 * NV, BF16), NPM=wa.carve(b + 6144 + 4352, 1024, F32),
                       INT=wa.carve(b + 6144 + 4352 + 4096, 1024, F32)))
    o2 = o1 + 3 * SETB
    NPMm = wa.carve(o2, 1024, F32)
    qpc = wa.carve(o2 + 4096, 1024, BF16)
    Vw = wa.carve(o2 + 6144, 8 * NV, BF16)
    DT = wa.carve(o2 + 10496, 1024, F32)
    scT = wa.carve(o2 + 14592, 1024, BF16)
    o3 = o2 + 16640
    hb = [wa.private(o3 + h * NVF * 4, NV, F32) for h in range(8)]
    hb_all = wa._ap(o3, 8 * NVF, F32, 128)[0]
    hout = wa.carve(o3 + 10240, 2048, F32)
    o4 = o3 + 10240 + 8192
    Vst = wa.carve(o4, 2048, BF16)
    assert o4 + 4096 <= wa.nbytes
    for s_ in range(3):
        p.op("pool", I("memset", LD[s_]["V"].t, 1.0), writes=LD[s_]["V"].k)
    st_cols, stt = statcols(c, "m_st", 96)
    mring = [Buf(stt[:, j * 8:(j + 1) * 8], [Tok("ms")]) for j in range(12)]
    stn = [0]

    def st8():
        j = stn[0]
        stn[0] += 1
        return mring[j % 12]
    qk_toks = [c.dtok("qk", j) for j in range(16)]
    hc = [0]
    steps = []
    for d in getattr(c, "ml_dirs", (0, 1)):
        order = list(range(16)) if d == 0 else list(range(15, -1, -1))
        for ch in order[:getattr(c, "ml_steps", 16)]:
            steps.append((d, ch))

    def loads(gs):
        d, ch = steps[gs]
        L = LD[gs % 3]
        cs = slice(ch * 128, (ch + 1) * 128)
        qT3, kT3, V3 = r3(L["qT"].t, 128), r3(L["kT"].t, 128), r3(L["V"].t, NV)
        p.op("sp", I("dma_start", out=L["qT"].t, in_=c.qT1[ch]), reads=qk_toks[0:8], writes=L["qT"].k, dma=True)
        p.op("sp", I("dma_start", out=L["kT"].t, in_=c.kT1[ch]), reads=qk_toks[8:16], writes=L["kT"].k, dma=True)
        p.op("sp", I("dma_start", out=L["k"].t, in_=c.k1[cs, :]),
             reads=[c.dtok("k", ch, 0), c.dtok("k", ch, 512)], writes=L["k"].k, dma=True)
        p.op("sp", I("dma_start", out=Vst.t, in_=c.v1[cs, :]),
             reads=[c.dtok("v", ch, q * 512) for q in range(4)], writes=Vst.k, dma=True)
        p.op("sp", I("dma_start", out=L["NPM"].t,
                     in_=bass.AP(c.npm_d.tensor, (d * 16 + ch) * 1024, [[0, 128], [1, 1024]])),
             reads=[c.dtok("npm", d)], writes=L["NPM"].k, dma=True)
        p.op("sp", I("dma_start", out=L["INT"].t,
                     in_=bass.AP(c.inter_d.tensor, (d * 16 + ch) * 1024, [[0, 128], [1, 1024]])),
             reads=[c.dtok("inter", d)], writes=L["INT"].k, dma=True)

    def vcopy(gs):
        L = LD[gs % 3]
        V3 = r3(L["V"].t, NV)
        p.op("act", I("activation", out=V3[:, :, 0:256], in_=r3(Vst.t, 256), func=AF.Copy), reads=Vst.k, writes=L["V"].k)

    def ctx(gs):
        d, ch = steps[gs]
        L = LD[gs % 3]
        return d, ch, L, r3(L["qT"].t, 128), r3(L["kT"].t, 128), r3(L["V"].t, NV), d * 128 + ch * 8

    def frontA(gs):
        d, ch, L, qT3, kT3, V3, cb0 = ctx(gs)
        mask = mnf if d == 0 else mnb
        p.op("dve", I("tensor_tensor", out=r3(NPMm.t, 128), in0=r3(L["NPM"].t, 128),
                      in1=mask.t.unsqueeze(1).broadcast_to([128, 8, 128]), op=ALU.add),
             reads=L["NPM"].k + mask.k, writes=NPMm.k)
        p.op("pe", seq(*[I("matmul", c.ps[h // 4].t[:, (h % 4) * 128:(h % 4 + 1) * 128], lhsT=kT3[:, h, :],
                           rhs=qT3[:, h, :], start=True, stop=True) for h in range(8)]),
             reads=L["qT"].k + L["kT"].k, writes=c.ps[0].k + c.ps[1].k)
        p.op("act", seq(*[I("activation", out=DT.t[:, h * 128:(h + 1) * 128], in_=r3(NPMm.t, 128)[:, h, :],
                            func=AF.Exp, bias=c.Gcol.t[:, cb0 + h:cb0 + h + 1]) for h in range(8)]),
             reads=NPMm.k + c.Gcol.k, writes=DT.k)

    def frontB(gs):
        d, ch, L, qT3, kT3, V3, cb0 = ctx(gs)
        p.op("dve", I("tensor_tensor", out=qpc.t, in0=L["qT"].t, in1=L["INT"].t, op=ALU.mult),
             reads=L["qT"].k + L["INT"].k, writes=qpc.k)
        p.op("pool", I("tensor_tensor", out=r3(Vw.t[:, 0:1024], 128), in0=r3(L["k"].t, 128),
                       in1=bc_mid(c.WScol.t[:, cb0:cb0 + 8], 128), op=ALU.mult),
             reads=L["k"].k + c.WScol.k, writes=Vw.k)
        for half in range(2):
            p.op("dve", I("tensor_tensor", out=scT.t[:, half * 512:(half + 1) * 512], in0=c.ps[half].t,
                          in1=DT.t[:, half * 512:(half + 1) * 512], op=ALU.mult),
                 reads=c.ps[half].k + DT.k, writes=scT.k)

    def compute(gs):
        d, ch, L, qT3, kT3, V3, cb0 = ctx(gs)
        if gs == 0 or steps[gs - 1][0] != d:
            for h in range(8):
                p.op("pool", I("memset", C32[h].t, 0.0), writes=C32[h].k)
                p.op("pool", I("memset", Cb[h].t, 0.0), writes=Cb[h].k)
        cs = slice(ch * 128, (ch + 1) * 128)
        Vw3 = r3(Vw.t, NV)
        if True:
            scT3, qpc3 = r3(scT.t, 128), r3(qpc.t, 128)
            for h in range(8):
                psN = c.ps[2 + (hc[0] % 3)]
                psU = c.ps[5 + (hc[0] % 2)]
                hc[0] += 1
                p.op("pe", seq(I("matmul", psN.t[:, 0:257], lhsT=scT3[:, h, :], rhs=V3[:, h, 0:257], start=True, stop=False),
                               I("matmul", psN.t[:, 0:257], lhsT=qpc3[:, h, :], rhs=Cb[h].t[:, 0:257], start=False, stop=True)),
                     reads=scT.k + L["V"].k + qpc.k + Cb[h].k, writes=psN.k)
                if h % 2 == 0:
                    p.op("act", I("activation", out=hb[h].t[:, 0:257], in_=psN.t[:, 0:257], func=AF.Copy),
                         reads=psN.k, writes=hb[h].k)
                else:
                    p.op("dve", I("tensor_copy", out=hb[h].t[:, 0:257], in_=psN.t[:, 0:257]),
                         reads=psN.k, writes=hb[h].k)
                p.op("pe", I("matmul", psU.t[:, 0:257], lhsT=Vw.t[:, h * 128:(h + 1) * 128], rhs=V3[:, h, 0:257],
                             start=True, stop=True), reads=L["V"].k + Vw.k, writes=psU.k)
                dcol = d * 128 + h * 16 + ch
                p.op("dve", I("scalar_tensor_tensor", out=C32[h].t[:, 0:257], in0=C32[h].t[:, 0:257],
                              scalar=c.decbc.t[:, dcol:dcol + 1], in1=psU.t[:, 0:257], op0=ALU.mult, op1=ALU.add),
                     reads=C32[h].k + c.decbc.k + psU.k, writes=C32[h].k)
                p.op("pool", I("tensor_copy", out=Cb[h].t[:, 0:257], in_=C32[h].t[:, 0:257]),
                     reads=C32[h].k, writes=Cb[h].k)
            if gs + 1 < len(steps):
                frontB(gs + 1)
            hb3 = r3(hb_all, NVF)
            hbk = [t for b in hb for t in b.k]
            dn, rc = st8(), st8()
            p.op("act", I("activation", out=dn.t, in_=hb3[:, :, 256], func=AF.Abs), reads=hbk, writes=dn.k)
            p.op("dve", I("tensor_tensor", out=dn.t, in0=dn.t, in1=c.ENcol.t[:, cb0:cb0 + 8], op=ALU.max),
                 reads=dn.k + c.ENcol.k, writes=dn.k)
            p.op("dve", I("reciprocal", out=rc.t, in_=dn.t), reads=dn.k, writes=rc.k)
            ho3 = r3(hout.t, 256)
            p.op("act", seq(*[I("activation", out=ho3[:, h, :], in_=hb3[:, h, 0:256], func=AF.Copy, scale=rc.t[:, h:h + 1])
                              for h in range(0, 8, 2)]), reads=hbk + rc.k, writes=hout.k)
            p.op("dve", I("tensor_tensor", out=ho3[:, 1::2, :], in0=hb3[:, 1::2, 0:256], in1=bc_mid(rc.t[:, 1::2], 256),
                          op=ALU.mult), reads=hbk + rc.k, writes=hout.k)
            dst = c.hF if d == 0 else c.hB
            p.op("sp", I("dma_start", out=dst[cs, :], in_=hout.t), reads=hout.k,
                 writes=[c.dtok("hF" if d == 0 else "hB", ch)], dma=True)

    pend = [None]
    loads(0)
    vcopy(0)
    if len(steps) > 1:
        loads(1)
    frontA(0)
    frontB(0)
    for gs in range(len(steps)):
        if gs + 1 < len(steps):
            vcopy(gs + 1)
        if gs + 2 < len(steps):
            loads(gs + 2)
        if gs + 1 < len(steps):
            frontA(gs + 1)
        compute(gs)
    wa.release()


def emit_final(c):
    nc, p, wa = c.nc, c.p, c.wa
    hmT, hmk = c.hmT, c.hmk
    wtok = [c.dtok("wout1", nb, half) for nb in range(4) for half in range(2)]
    K8 = 8192
    hFb = [wa.carve(0, 2048, F32), wa.carve(K8, 2048, F32)]
    hBb = [wa.carve(2 * K8, 2048, F32), wa.carve(3 * K8, 2048, F32)]
    osb = [wa.carve(4 * K8, 2048, F32), wa.carve(5 * K8, 2048, F32)]
    zsb = [wa.carve(6 * K8, 2048, F32), wa.carve(7 * K8, 2048, F32)]
    x1tb = [wa.carve(8 * K8, 2048, F32), wa.carve(15 * K8, 2048, F32)]
    x2 = wa.carve(9 * K8, 2048, F32)
    yb = wa.carve(10 * K8, 2048, F32)
    fbc = wa.carve(11 * K8, 2048, F32)
    junk = wa.carve(12 * K8, 2048, BF16)
    hsb = wa.carve(12 * K8 + 4096, 2048, BF16)
    mt = [wa.carve(13 * K8, 2048, BF16), wa.carve(13 * K8 + 4096, 2048, BF16), wa.carve(14 * K8, 2048, BF16)]
    p.op("sp", I("dma_start", out=fbc.t, in_=c.fgain.broadcast_to([128, D])), writes=fbc.k, dma=True)
    sc, sct = statcols(c, "F_st", 64 + 16 * 32)
    fused = ("x1", 0, 0) in c.dtoks

    def loads(i):
        cs = slice(i * 128, (i + 1) * 128)
        p.op("sp", I("dma_start", out=hFb[i % 2].t, in_=c.hF[cs, :]), reads=[c.dtok("hF", i)], writes=hFb[i % 2].k, dma=True)
        p.op("sp", I("dma_start", out=hBb[i % 2].t, in_=c.hB[cs, :]), reads=[c.dtok("hB", i)], writes=hBb[i % 2].k, dma=True)
        p.op("sp", I("dma_start", out=osb[i % 2].t, in_=c.osig[cs, :]), reads=[c.dtok("o", i, q * 512) for q in range(4)],
             writes=osb[i % 2].k, dma=True)
        p.op("sp", I("dma_start", out=zsb[i % 2].t, in_=c.zsil[cs, :]), reads=[c.dtok("z", i, q * 512) for q in range(4)],
             writes=zsb[i % 2].k, dma=True)

    fst = [[Buf(sct[:, 64 + i * 32 + 8 * q:64 + i * 32 + 8 * q + 8], [Tok("fs")]) for q in range(4)] for i in range(NT)]

    def finalize(i):
        hF_, hB_, os_, zs_ = hFb[i % 2], hBb[i % 2], osb[i % 2], zsb[i % 2]
        p.op("dve", I("tensor_tensor", out=hB_.t, in0=hB_.t, in1=hF_.t, op=ALU.add), reads=hB_.k + hF_.k, writes=hB_.k)
        p.op("pool", I("tensor_tensor", out=hB_.t, in0=hB_.t, in1=os_.t, op=ALU.mult), reads=hB_.k + os_.k, writes=hB_.k)
        p.op("act", I("activation", out=hF_.t, in_=hB_.t, func=AF.Square), reads=hB_.k, writes=hF_.k)

    def fin2(i):
        hF_, hB_, os_, zs_ = hFb[i % 2], hBb[i % 2], osb[i % 2], zsb[i % 2]
        ss, rs, rs2, rstd = fst[i]
        p.op("dve", I("tensor_reduce", out=ss.t, in_=r3(hF_.t, 256), axis=AX.X, op=ALU.add), reads=hF_.k, writes=ss.k)
        p.op("dve", I("tensor_scalar", out=rs.t, in0=ss.t, scalar1=1.0 / 256, scalar2=EPS, op0=ALU.mult, op1=ALU.add),
             reads=ss.k, writes=rs.k)
        p.op("act", I("activation", out=rs2.t, in_=rs.t, func=AF.Sqrt), reads=rs.k, writes=rs2.k)
        p.op("dve", I("reciprocal", out=rstd.t, in_=rs2.t), reads=rs2.k, writes=rstd.k)
        ho3 = r3(hB_.t, 256)
        p.op("act", seq(*[I("activation", out=ho3[:, h, :], in_=ho3[:, h, :], func=AF.Copy, scale=rstd.t[:, h:h + 1])
                          for h in range(8)]), reads=hB_.k + rstd.k, writes=hB_.k)
        p.op("dve", I("tensor_tensor", out=hsb.t, in0=hB_.t, in1=zs_.t, op=ALU.mult), reads=hB_.k + zs_.k, writes=hsb.k)

    def fin_tr(i):
        m_ = mt[i % 3]
        for g in range(4):
            pb = c.ps[6 + (g % 2)]
            pT = pb.t.bitcast(BF16)
            p.op("pe", seq(*[I("transpose", out=pT[:, q * 128:(q + 1) * 128],
                               in_=hsb.t[:, (4 * g + q) * 128:(4 * g + q + 1) * 128], identity=c.cidb.t)
                             for q in range(4)]), reads=hsb.k + c.cidb.k, writes=pb.k)
            p.op("act", I("activation", out=r3(m_.t, 128)[:, 4 * g:4 * g + 4, :], in_=r3(pT[:, 0:512], 128), func=AF.Copy),
                 reads=pb.k, writes=m_.k)

    def x1load(i):
        x1t = x1tb[i % 2]
        p.op("sp", I("dma_start", out=x1t.t, in_=c.x1[i * 128:(i + 1) * 128, :]),
             reads=([c.dtok("x1", i, nb) for nb in range(4)] if fused else []), writes=x1t.k, dma=True)

    def project(i):
        m3 = r3(mt[i % 3].t, 128)
        for nb in range(4):
            psb = c.ps[(4 * i + nb) % 6]
            p.op("pe", seq(*[I("matmul", psb.t, lhsT=m3[:, ch, :], rhs=hmT[:, ch, nb * 512:(nb + 1) * 512],
                               start=(ch == 0), stop=(ch == 15)) for ch in range(16)]),
                 reads=mt[i % 3].k + wtok, writes=psb.k)

    def proj_post(i):
        x1t = x1tb[i % 2]
        ss, rs, rs2, rstd = sc[4 * i:4 * i + 4]
        for nb in range(4):
            psb = c.ps[(4 * i + nb) % 6]
            p.op("dve", I("tensor_tensor", out=x2.t[:, nb * 512:(nb + 1) * 512], in0=psb.t,
                          in1=x1t.t[:, nb * 512:(nb + 1) * 512], op=ALU.add), reads=psb.k + x1t.k, writes=x2.k)
        p.op("act", I("activation", out=junk.t, in_=x2.t, func=AF.Square, accum_out=ss.t), reads=x2.k, writes=junk.k + ss.k)
        p.op("dve", I("tensor_scalar", out=rs.t, in0=ss.t, scalar1=1.0 / D, scalar2=EPS, op0=ALU.mult, op1=ALU.add),
             reads=ss.k, writes=rs.k)
        p.op("act", I("activation", out=rs2.t, in_=rs.t, func=AF.Sqrt), reads=rs.k, writes=rs2.k)
        p.op("dve", I("reciprocal", out=rstd.t, in_=rs2.t), reads=rs2.k, writes=rstd.k)
        p.op("act", I("activation", out=yb.t, in_=x2.t, func=AF.Copy, scale=rstd.t), reads=x2.k + rstd.k, writes=yb.k)
        p.op("pool", I("tensor_tensor", out=yb.t, in0=yb.t, in1=fbc.t, op=ALU.mult), reads=yb.k + fbc.k, writes=yb.k)
        p.op("pool", I("dma_start", out=c.out[i * 128:(i + 1) * 128, :], in_=yb.t), reads=yb.k, writes=[c.dtok("out", i)], dma=True)

    loads(0)
    loads(1)
    finalize(0)
    fin2(0)
    fin_tr(0)
    loads(2)
    finalize(1)
    fin2(1)
    fin_tr(1)
    loads(3)
    x1load(0)
    for i in range(NT):
        if i + 1 < NT:
            x1load(i + 1)
        project(i)
        if 2 <= i + 1 < NT:
            fin_tr(i + 1)
        if i + 2 < NT:
            finalize(i + 2)
        proj_post(i)
        if i + 2 < NT:
            fin2(i + 2)
        if i + 4 < NT:
            loads(i + 4)
```
